# Optimizing a Trainium2 kernel written in Bass

```python
import jax
import jax.numpy as jnp
from jax import lax
import numpy as np

D_MODEL = 4096
BATCH = 8
SEQ = 2048
DEPTH = 1
DEC_BATCH = 8
DEC_SEQ = 32
PAST_LEN = 1024

CHUNK = 64
M_HEADS = 8
M_DQK = 128
M_DV = 256
A_HEADS = 32
A_KV_HEADS = 4
A_GROUP = A_HEADS // A_KV_HEADS
A_HD = 64
WINDOW = 128
WIN_CHUNKS = WINDOW // CHUNK
ROPE_THETA = 10000.0
D_FF = 4 * D_MODEL
N_BRANCH = 2
ALPHA = (2.0 * DEPTH) ** 0.25
BETA = (8.0 * DEPTH) ** -0.25
LN_EPS = 1e-5
RMS_EPS = 1e-6

M_QK_W = M_HEADS * M_DQK
M_V_W = M_HEADS * M_DV
A_Q_W = A_HEADS * A_HD
A_KV_W = A_KV_HEADS * A_HD
IN_SPLITS = (M_QK_W, M_QK_W, M_V_W, M_V_W, M_HEADS, M_HEADS, A_Q_W, A_KV_W, A_KV_W, N_BRANCH * D_MODEL)
D_IN = sum(IN_SPLITS)

kernel_name = 'hybrid_mlstm_swa_sink_stream_step'


def _layer_norm(x, g, b):
    xf = x.astype(jnp.float32)
    mu = jnp.mean(xf, -1, keepdims=True)
    var = jnp.mean(jnp.square(xf - mu), -1, keepdims=True)
    return ((xf - mu) * lax.rsqrt(var + LN_EPS) * g + b).astype(x.dtype)


def _split_in(z):
    parts, off = [], 0
    for n in IN_SPLITS:
        parts.append(z[..., off:off + n])
        off += n
    return parts


def _rope(x, pos):
    half = A_HD // 2
    inv = ROPE_THETA ** (-jnp.arange(half, dtype=jnp.float32) / half)
    ang = pos.astype(jnp.float32)[:, None] * inv[None, :]
    cos = jnp.cos(ang)[:, None, :]
    sin = jnp.sin(ang)[:, None, :]
    xf = x.astype(jnp.float32)
    x1, x2 = xf[..., :half], xf[..., half:]
    return jnp.concatenate([x1 * cos - x2 * sin, x2 * cos + x1 * sin], -1).astype(x.dtype)


def _mlstm_prep(q, k, v, i_pre, f_pre, b_ig, b_fg):
    B, S = q.shape[:2]
    heads = lambda t, d: t.reshape(B, S, M_HEADS, d).transpose(0, 2, 1, 3).astype(jnp.float32)
    qh = heads(q, M_DQK) * (M_DQK ** -0.5)
    kh = heads(k, M_DQK)
    vh = heads(v, M_DV)
    ig = (i_pre.astype(jnp.float32) + b_ig).transpose(0, 2, 1)
    lf = jax.nn.log_sigmoid(f_pre.astype(jnp.float32) + b_fg).transpose(0, 2, 1)
    return qh, kh, vh, ig, lf


def _mlstm_chunk(carry, inp):
    C, n_st, m = carry
    q, k, v, ig, lf = inp
    L = q.shape[2]
    b = jnp.cumsum(lf, axis=-1)
    causal = jnp.tril(jnp.ones((L, L), bool))
    dlog = jnp.where(causal, b[..., :, None] - b[..., None, :] + ig[..., None, :], -jnp.inf)
    inter = b + m[..., None]
    m_t = jnp.maximum(inter, jnp.max(dlog, -1))
    s = jnp.einsum('bhtd,bhsd->bhts', q, k) * jnp.exp(dlog - m_t[..., None])
    a = jnp.exp(inter - m_t)
    num = a[..., None] * jnp.einsum('bhvd,bhtd->bhtv', C, q) + jnp.einsum('bhts,bhsv->bhtv', s, v)
    den = a * jnp.einsum('bhd,bhtd->bht', n_st, q) + jnp.sum(s, -1)
    h = num / jnp.maximum(jnp.abs(den), jnp.exp(-m_t))[..., None]
    bL = b[..., -1]
    g_log = bL[..., None] - b + ig
    m_new = jnp.maximum(bL + m, jnp.max(g_log, -1))
    wk = jnp.exp(g_log - m_new[..., None])
    decay = jnp.exp(bL + m - m_new)
    C_new = decay[..., None, None] * C + jnp.einsum('bhs,bhsv,bhsd->bhvd', wk, v, k)
    n_new = decay[..., None] * n_st + jnp.einsum('bhs,bhsd->bhd', wk, k)
    return (C_new, n_new, m_new), h


def _mlstm_prompt(qh, kh, vh, ig, lf):
    B, H, S, _ = qh.shape
    nC = S // CHUNK
    blk = lambda t: jnp.moveaxis(t.reshape(B, H, nC, CHUNK, *t.shape[3:]), 2, 0)
    C0 = jnp.zeros((B, H, M_DV, M_DQK), jnp.float32)
    n0 = jnp.zeros((B, H, M_DQK), jnp.float32)
    m0 = jnp.zeros((B, H), jnp.float32)
    (C, n_st, m), hs = lax.scan(_mlstm_chunk, (C0, n0, m0), (blk(qh), blk(kh), blk(vh), blk(ig), blk(lf)))
    h = jnp.moveaxis(hs, 0, 2).reshape(B, H, S, M_DV)
    return h, C, n_st, m


def _mlstm_out(h, o_pre, w_mnorm, dtype):
    B, _, S, _ = h.shape
    hn = h * lax.rsqrt(jnp.mean(jnp.square(h), -1, keepdims=True) + RMS_EPS)
    hn = hn.transpose(0, 2, 1, 3).reshape(B, S, M_V_W) * w_mnorm
    return (jax.nn.sigmoid(o_pre.astype(jnp.float32)) * hn).astype(dtype)


def _sink_probs(s, sink):
    sk = sink.astype(jnp.float32).reshape(A_KV_HEADS, A_GROUP, 1)
    mx = jnp.maximum(jnp.max(s, -1), sk)
    p = jnp.exp(s - mx[..., None])
    return p / (jnp.sum(p, -1) + jnp.exp(sk - mx))[..., None]


def _swa_prompt(q, k, v, sink):
    B, S = q.shape[:2]
    nC = S // CHUNK
    qb = q.reshape(B, nC, CHUNK, A_KV_HEADS, A_GROUP, A_HD)

    def bands(t):
        tc = t.reshape(B, nC, CHUNK, A_KV_HEADS, A_HD)
        tp = jnp.concatenate([jnp.zeros((B, WIN_CHUNKS, CHUNK, A_KV_HEADS, A_HD), t.dtype), tc], axis=1)
        return jnp.concatenate([tp[:, j:j + nC] for j in range(WIN_CHUNKS + 1)], axis=2)

    kb, vb = bands(k), bands(v)
    src_chunk = jnp.arange(nC)[:, None] + jnp.arange(WIN_CHUNKS + 1)[None, :] - WIN_CHUNKS
    valid = jnp.repeat(src_chunk >= 0, CHUNK, axis=1)
    s = jnp.einsum('bcqkgd,bcjkd->bckgqj', qb, kb, preferred_element_type=jnp.float32) * (A_HD ** -0.5)
    s = jnp.where(valid[None, :, None, None, None, :], s, -jnp.inf)
    p = _sink_probs(s, sink)
    o = jnp.einsum('bckgqj,bcjkd->bcqkgd', p.astype(v.dtype), vb)
    return o.reshape(B, S, A_Q_W)


def _swa_sample(q, k_all, v_all, sink):
    B, T = q.shape[:2]
    qg = q.reshape(B, T, A_KV_HEADS, A_GROUP, A_HD)
    s = jnp.einsum('btkgd,bjkd->bkgtj', qg, k_all, preferred_element_type=jnp.float32) * (A_HD ** -0.5)
    p = _sink_probs(s, sink)
    o = jnp.einsum('bkgtj,bjkd->btkgd', p.astype(v_all.dtype), v_all)
    return o.reshape(B, T, A_Q_W)


def _mixer_inputs(x, pos, w_in, b_ig, b_fg):
    B, S, _ = x.shape
    z = jnp.einsum('bsd,de->bse', x, w_in)
    mq, mk, mv, mo, mi, mf, aq, ak, av, gp = _split_in(z)
    mlstm = _mlstm_prep(mq, mk, mv, mi, mf, b_ig, b_fg)
    aq = _rope(aq.reshape(B, S, A_HEADS, A_HD), pos)
    ak = _rope(ak.reshape(B, S, A_KV_HEADS, A_HD), pos)
    av = av.reshape(B, S, A_KV_HEADS, A_HD)
    return mlstm, mo, aq, ak, av, gp


def _finish(x, ya, yb, gp, w_br_a, w_br_b, w_out, ln1_g, ln1_b, w_up, w_down, ln2_g, ln2_b):
    B, S, _ = x.shape
    g = jax.nn.sigmoid(gp.astype(jnp.float32)).reshape(B, S, N_BRANCH, D_MODEL).astype(x.dtype)
    merged = g[:, :, 0] * (ya @ w_br_a) + g[:, :, 1] * (yb @ w_br_b)
    h = _layer_norm(ALPHA * x + merged @ w_out, ln1_g, ln1_b)
    f = jnp.square(jax.nn.relu(h @ w_up)) @ w_down
    return _layer_norm(ALPHA * h + f, ln2_g, ln2_b)


def setup_inputs(seed: int = 0) -> dict:
    key = jax.random.key(seed)
    ks = jax.random.split(key, 24)
    nrm = lambda k, shape, scale: scale * jax.random.normal(k, shape, jnp.float32)
    L = DEPTH
    return {
        'x_prompt': nrm(ks[0], (BATCH, SEQ, D_MODEL), 1.0),
        'x_sample': nrm(ks[1], (DEC_BATCH, DEC_SEQ, D_MODEL), 1.0),
        'cache_swa_k': nrm(ks[2], (L, DEC_BATCH, WINDOW, A_KV_HEADS, A_HD), 1.0),
        'cache_swa_v': nrm(ks[3], (L, DEC_BATCH, WINDOW, A_KV_HEADS, A_HD), 1.0),
        'state_mlstm_C': nrm(ks[4], (L, DEC_BATCH, M_HEADS, M_DV, M_DQK), 0.3),
        'state_mlstm_n': nrm(ks[5], (L, DEC_BATCH, M_HEADS, M_DQK), 0.3),
        'state_mlstm_m': nrm(ks[6], (L, DEC_BATCH, M_HEADS), 0.5),
        'w_in': nrm(ks[7], (L, D_MODEL, D_IN), D_MODEL ** -0.5),
        'b_igate': nrm(ks[8], (L, M_HEADS), 0.1),
        'b_fgate': jnp.linspace(3.0, 6.0, M_HEADS)[None, :] + nrm(ks[9], (L, M_HEADS), 0.1),
        'w_mnorm': 1.0 + nrm(ks[10], (L, M_V_W), 0.05),
        'attn_sink': nrm(ks[11], (L, A_HEADS), 0.5),
        'w_branch_a': nrm(ks[12], (L, M_V_W, D_MODEL), BETA * M_V_W ** -0.5),
        'w_branch_b': nrm(ks[13], (L, A_Q_W, D_MODEL), BETA * A_Q_W ** -0.5),
        'w_out': nrm(ks[14], (L, D_MODEL, D_MODEL), BETA * D_MODEL ** -0.5),
        'ln1_g': 1.0 + nrm(ks[15], (L, D_MODEL), 0.05),
        'ln1_b': nrm(ks[16], (L, D_MODEL), 0.02),
        'w_up': nrm(ks[17], (L, D_MODEL, D_FF), D_MODEL ** -0.5),
        'w_down': nrm(ks[18], (L, D_FF, D_MODEL), BETA * D_FF ** -0.5),
        'ln2_g': 1.0 + nrm(ks[19], (L, D_MODEL), 0.05),
        'ln2_b': nrm(ks[20], (L, D_MODEL), 0.02),
    }


def reference(x_prompt, x_sample, cache_swa_k, cache_swa_v, state_mlstm_C, state_mlstm_n, state_mlstm_m,
              w_in, b_igate, b_fgate, w_mnorm, attn_sink, w_branch_a, w_branch_b, w_out,
              ln1_g, ln1_b, w_up, w_down, ln2_g, ln2_b):
    S = x_prompt.shape[1]
    T = x_sample.shape[1]
    pos_p = jnp.arange(S, dtype=jnp.int32)
    pos_s = PAST_LEN + jnp.arange(T, dtype=jnp.int32)
    xp, xs = x_prompt, x_sample
    pk, pv, pC, pn, pm = [], [], [], [], []
    sk, sv, sC, sn, sm = [], [], [], [], []
    for l in range(DEPTH):
        (qh, kh, vh, ig, lf), mo, aq, ak, av, gp = _mixer_inputs(xp, pos_p, w_in[l], b_igate[l], b_fgate[l])
        h, C, n_st, m = _mlstm_prompt(qh, kh, vh, ig, lf)
        ya = _mlstm_out(h, mo, w_mnorm[l], xp.dtype)
        yb = _swa_prompt(aq, ak, av, attn_sink[l])
        pk.append(ak[:, -WINDOW:])
        pv.append(av[:, -WINDOW:])
        pC.append(C)
        pn.append(n_st)
        pm.append(m)
        xp = _finish(xp, ya, yb, gp, w_branch_a[l], w_branch_b[l], w_out[l],
                     ln1_g[l], ln1_b[l], w_up[l], w_down[l], ln2_g[l], ln2_b[l])
        (qs, kss, vs, igs, lfs), mo_s, aq_s, ak_s, av_s, gp_s = _mixer_inputs(xs, pos_s, w_in[l], b_igate[l], b_fgate[l])
        carry0 = (state_mlstm_C[l].astype(jnp.float32), state_mlstm_n[l].astype(jnp.float32),
                  state_mlstm_m[l].astype(jnp.float32))
        (C_s, n_s, m_s), h_s = _mlstm_chunk(carry0, (qs, kss, vs, igs, lfs))
        ya_s = _mlstm_out(h_s, mo_s, w_mnorm[l], xs.dtype)
        k_all = jnp.concatenate([cache_swa_k[l].astype(ak_s.dtype), ak_s], axis=1)
        v_all = jnp.concatenate([cache_swa_v[l].astype(av_s.dtype), av_s], axis=1)
        yb_s = _swa_sample(aq_s, k_all, v_all, attn_sink[l])
        sk.append(ak_s)
        sv.append(av_s)
        sC.append(C_s)
        sn.append(n_s)
        sm.append(m_s)
        xs = _finish(xs, ya_s, yb_s, gp_s, w_branch_a[l], w_branch_b[l], w_out[l],
                     ln1_g[l], ln1_b[l], w_up[l], w_down[l], ln2_g[l], ln2_b[l])
    return (xp, xs, jnp.stack(pk), jnp.stack(pv), jnp.stack(pC), jnp.stack(pn), jnp.stack(pm),
            jnp.stack(sk), jnp.stack(sv), jnp.stack(sC), jnp.stack(sn), jnp.stack(sm))
```

```python
import os
from contextlib import ExitStack
import numpy as np
import ml_dtypes
import concourse.bass as bass
import concourse.mybir as mybir
from concourse.bass_utils import run_bass_kernel_spmd

F32 = mybir.dt.float32
BF16 = mybir.dt.bfloat16
ALU = mybir.AluOpType
AF = mybir.ActivationFunctionType
AX = mybir.AxisListType

NTOK = 2080
S = 2048
TS = 32
D = 4096
DFF = 16384
ALPHA = 2.0 ** 0.25
LN_EPS = 1e-5
RMS_EPS = 1e-6
ENGS = ("sync", "scalar", "gpsimd", "vector", "tensor")
HALVES = ((0, 1024), (1024, 1056))
TBLOCKS = [(i * 128, 128) for i in range(16)] + [(2048, 32)]


def ttiles(t0, n):
    out = []
    t = t0
    while t < t0 + n:
        w = min(512, t0 + n - t)
        out.append((t, w))
        t += w
    return out


class Ev:
    __slots__ = ("sem", "val", "needed", "dma")

    def __init__(self):
        self.sem = None
        self.val = 0
        self.needed = False
        self.dma = False


class Buf:
    def __init__(self, name=""):
        self.name = name
        self.w = []
        self.r = []
        self.prev = []

    def new_gen(self):
        self.prev = list(self.r) if self.r else list(self.w)
        self.w = []
        self.r = []


class DSem:
    def __init__(self, K, name):
        self.K = K
        self.name = name
        self._sem = None
        self.count = 0

    @property
    def sem(self):
        if self._sem is None:
            self._sem = self.K.new_sem(self.name)
        return self._sem


class Op:
    __slots__ = ("fn", "waits", "ev")


class Kern:
    def __init__(self, nc, stack):
        self.nc = nc
        self.stack = stack
        self.nsem = 0
        self.phase_idx = 0
        self.pool = []
        self.bar = self.new_sem("bar")
        self.dsems = []
        self.esem = {e: self.new_sem(f"eng_{e}") for e in ENGS if e != "sync"}
        self.ecount = {e: 0 for e in ENGS}

    def new_sem(self, name):
        if self.pool:
            return self.pool.pop()
        self.nsem += 1
        return self.stack.enter_context(self.nc.semaphore(f"{name}_{self.nsem}"))

    def new_dsem(self, name="d"):
        d = DSem(self, name)
        self.dsems.append(d)
        return d


class Phase:
    def __init__(self, K, name):
        self.K = K
        self.nc = K.nc
        self.name = name
        self.q = {e: [] for e in ENGS}
        self.esem = K.esem
        self.bank_i = 0

    def op(self, eng, fn, reads=(), writes=(), pwrites=(), dsem=None):
        o = Op()
        o.fn = fn
        ev = Ev()
        waits = []
        for b in reads:
            waits += b.w
        for b in writes:
            b.new_gen()
            waits += b.prev
        for b in pwrites:
            waits += b.prev
        seen = set()
        o.waits = []
        for w in waits:
            if id(w) not in seen:
                seen.add(id(w))
                w.needed = True
                o.waits.append(w)
        if dsem is not None:
            dsem.count += 16
            ev.sem = dsem.sem
            ev.val = dsem.count
            ev.dma = True
            ev.needed = True
        o.ev = ev
        for b in reads:
            b.r.append(ev)
        for b in writes:
            b.w.append(ev)
        for b in pwrites:
            b.w.append(ev)
        self.q[eng].append(o)
        return ev

    def dma(self, eng, out, in_, dsem, reads=(), writes=(), pwrites=()):
        return self.op(eng, lambda e: e.dma_start(out=out, in_=in_), reads, writes, pwrites, dsem)

    def run(self):
        K = self.K
        nc = self.nc
        for eng in ENGS:
            if eng == "sync":
                continue
            q = self.q[eng]
            last = None
            for o in q:
                if not o.ev.dma:
                    last = o
            if last is not None:
                last.ev.needed = True
            cnt = K.ecount[eng]
            for o in q:
                if o.ev.dma:
                    continue
                if o.ev.needed:
                    cnt += 1
                    o.ev.sem = self.esem[eng]
                    o.ev.val = cnt
            K.ecount[eng] = cnt
            self._final = getattr(self, "_final", {})
            self._final[eng] = (self.esem[eng], cnt)
        pidx = K.phase_idx
        finals = [(s, v) for (s, v) in self._final.values() if v > 0]
        used = [d for d in K.dsems if d.count > 0]
        dfinals = [(d.sem, d.count) for d in used]
        K.dsems = []
        with nc.Block() as block:
            for eng in ENGS:
                def body(e, eng=eng):
                    known = {}
                    if pidx > 0:
                        e.wait_ge(K.bar, pidx)
                    for o in self.q[eng]:
                        for w in o.waits:
                            if known.get(id(w.sem), 0) < w.val:
                                e.wait_ge(w.sem, w.val)
                                known[id(w.sem)] = w.val
                        ins = o.fn(e)
                        if o.ev.needed:
                            ins.then_inc(o.ev.sem, 16 if o.ev.dma else 1)
                    if eng == "sync":
                        for (s_, v_) in finals + dfinals:
                            e.wait_ge(s_, v_)
                        for (s_, v_) in dfinals:
                            e.sem_clear(s_)
                        e.sem_inc(K.bar, 1)
                        e.wait_ge(K.bar, pidx + 1)
                getattr(block, eng)(body)
        K.phase_idx += 1
        for d in used:
            K.pool.append(d._sem)


def build(debug=False, cfg=None):
    cfg = cfg or {}
    nc = bass.Bass("TRN2", target_bir_lowering=False)
    stack = ExitStack()
    K = Kern(nc, stack)

    def din(name, shape, dt=F32):
        return nc.dram_tensor(name, list(shape), dt, kind="ExternalInput").ap()

    def dout(name, shape, dt=F32):
        return nc.dram_tensor(name, list(shape), dt, kind="ExternalOutput").ap()

    def dscr(name, shape, dt=BF16):
        return nc.dram_tensor(name, list(shape), dt, kind="ExternalOutput" if debug else "Internal").ap()

    x = din("x", [NTOK, D])
    w1f = din("w1f", [D, 12800])
    wg = din("wg", [D, 16])
    w1t = din("w1t", [D, 4352])
    bigf = din("bigf", [8, 2])
    wmn = din("wmn", [1, 2048])
    wa = din("wa", [2048, D])
    wb = din("wb", [2048, D])
    wo = din("wo", [D, D])
    wu = din("wu", [D, DFF])
    wd = din("wd", [DFF, D])
    lnp = din("lnp", [4, D])
    sinkr = din("sinkr", [1, 32])
    ck = din("ck", [128, 256])
    cv = din("cv", [128, 256])
    identf = din("identf", [128, 128])
    identb = din("identb", [128, 128], BF16)
    permb = din("permb", [128, 128], BF16)
    maskT = din("maskT", [128, 128])
    cosT = din("cosT", [128, NTOK])
    sinT = din("sinT", [128, NTOK])
    stC = din("stC", [8, 256, 128])
    stn = din("stn", [8, 128])
    stm = din("stm", [8, 1])
    y = dout("y", [NTOK, D])
    pk = dout("pk", [128, 256])
    pv = dout("pv", [128, 256])
    sk = dout("sk", [32, 256])
    sv = dout("sv", [32, 256])
    pC = dout("pC", [8, 256, 128])
    pn = dout("pn", [8, 128])
    pm = dout("pm", [8, 1])
    sC = dout("sC", [8, 256, 128])
    sn = dout("sn", [8, 128])
    sm = dout("sm", [8, 1])
    QT = dscr("QT", [8, 128, NTOK])
    KT = dscr("KT", [8, 128, NTOK])
    AQR = dscr("AQR", [16, 128, NTOK])
    AKR = dscr("AKR", [4, 128, NTOK])
    GT = dscr("GT", [64, 128, NTOK])
    IG = dscr("IG", [8, NTOK], F32)
    FP = dscr("FP", [8, NTOK], F32)
    MV = dscr("MV", [NTOK, 2048])
    SIGMO = dscr("SIGMO", [NTOK, 2048])
    AV = dscr("AV", [NTOK, 256])
    YT = dscr("YT", [32, 128, NTOK])
    HPRE = dscr("HPRE", [NTOK, D], F32)
    HH = dscr("HH", [NTOK, D], F32)
    HT = dscr("HT", [32, 128, NTOK])
    UT = dscr("UT", [5, 128, 128, 512])
    FPRE = dscr("FPRE", [NTOK, D], F32)

    ps_t = stack.enter_context(nc.psum_tensor("ps", [128, 8, 512], F32))
    banks = [Buf(f"bank{i}") for i in range(8)]

    def sb(st, name, shape, dt):
        return st.enter_context(nc.sbuf_tensor(name, list(shape), dt))

    def wview(w, c0, ncol, k0=0, nk=32):
        return w.rearrange("(k p) n -> p k n", p=128)[:, k0:k0 + nk, c0:c0 + ncol]

    def X(P, eng, meth, args, reads=(), writes=(), pwrites=(), **kw):
        return P.op(eng, lambda e: getattr(e, meth)(*args, **kw), reads, writes, pwrites)

    def MM(P, bi, out_ap, pairs, reads):
        np_ = len(pairs)

        def fn(e):
            ins = None
            for i, (l, r) in enumerate(pairs):
                ins = e.matmul(out_ap, l, r, start=(i == 0), stop=(i == np_ - 1))
            return ins
        return P.op("tensor", fn, reads=reads, writes=[banks[bi]])

    def nbank(P):
        i = P.bank_i % 8
        P.bank_i += 1
        return i

    class Ring:
        def __init__(self, st, name, n, shape, dt):
            self.t = [sb(st, f"{name}{i}", shape, dt) for i in range(n)]
            self.b = [Buf(f"{name}{i}") for i in range(n)]
            self.d = [K.new_dsem(name) for i in range(n)]
            self.i = 0
            self.n = n

        def nxt(self):
            i = self.i % self.n
            self.i += 1
            return self.t[i], self.b[i], self.d[i]

    def phase1():
        P = Phase(K, "p1")
        st = ExitStack()
        xT = sb(st, "xT", [128, 32, 1056], BF16)
        XS = Ring(st, "xs", 2, [128, D], F32)
        WR = Ring(st, "wr", 2, [128, 32, 512], BF16)
        STB = Ring(st, "stb", 4, [128, 512], BF16)
        STF = Ring(st, "stf", 4, [128, 512], F32)
        wgt = sb(st, "wgt", [128, 32, 16], BF16)
        cst = sb(st, "cst", [128, 1056], F32)
        snt = sb(st, "snt", [128, 1056], F32)
        idf = sb(st, "idf", [128, 128], F32)
        bg = sb(st, "bg", [8, 2], F32)
        pmb = sb(st, "pmb", [128, 128], BF16)
        B_xT = Buf("xT")
        B_c = Buf("consts")
        B_tab = Buf("tab")
        D_c = K.new_dsem()

        P.dma("sync", idf[:], identf[:, :], K.new_dsem(), pwrites=[B_c])
        P.dma("sync", bg[:], bigf[:, :], K.new_dsem(), pwrites=[B_c])
        P.dma("sync", pmb[:], permb[:, :], K.new_dsem(), pwrites=[B_c])
        P.dma("gpsimd", wgt[:], wg.rearrange("(k p) n -> p k n", p=128), K.new_dsem(), pwrites=[B_c])

        for (h0, hn) in cfg.get('halves', HALVES):
            P.dma("sync", cst[:, 0:hn], cosT[:, h0:h0 + hn], K.new_dsem(), writes=[B_tab])
            P.dma("sync", snt[:, 0:hn], sinT[:, h0:h0 + hn], K.new_dsem(), pwrites=[B_tab])
            B_xT.new_gen()
            blocks = [(t, n) for (t, n) in TBLOCKS if h0 <= t < h0 + hn]
            for (t, n) in blocks:
                xs, bxs, dxs = XS.nxt()
                P.dma("sync", xs[0:n, :], x[t:t + n, :], dxs, writes=[bxs])
                for c0 in range(0, 32, 4):
                    bi = nbank(P)
                    trs = [(ps_t[:, bi, j * n:(j + 1) * n], xs[0:n, (c0 + j) * 128:(c0 + j + 1) * 128], idf[0:n, 0:n])
                           for j in range(4)]

                    def fn(e, trs=trs):
                        ins = None
                        for (o_, i_, d_) in trs:
                            ins = e.transpose(o_, i_, d_)
                        return ins
                    P.op("tensor", fn, reads=[bxs, B_c], writes=[banks[bi]])
                    src = ps_t[:, bi, 0:4 * n].rearrange("p (j n) -> p j n", j=4)
                    dst = xT[:, c0:c0 + 4, t - h0:t - h0 + n]
                    if (c0 // 4) % 2 == 0:
                        X(P, "vector", "tensor_copy", (dst, src), reads=[banks[bi]], pwrites=[B_xT])
                    else:
                        X(P, "scalar", "copy", (dst, src), reads=[banks[bi]], pwrites=[B_xT])
            tts = ttiles(h0, hn)

            def fm_group(wtile, bw, col0, m, t, n):
                bi = nbank(P)
                pairs = [(wtile[:, k, col0:col0 + m], xT[:, k, t - h0:t - h0 + n]) for k in range(32)]
                MM(P, bi, ps_t[0:m, bi, 0:n], pairs, reads=[bw, B_xT])
                return bi

            for gi, dst_d in cfg.get('gates', ((0, IG), (1, FP))):
                for (t, n) in tts:
                    bi = fm_group(wgt, B_c, gi * 8, 8, t, n)
                    s_, bs, ds = STF.nxt()
                    X(P, "vector", "tensor_scalar", (s_[0:8, 0:n], ps_t[0:8, bi, 0:n], bg[0:8, gi:gi + 1], None, ALU.add),
                      reads=[banks[bi], B_c], writes=[bs])
                    P.dma("sync", dst_d[:, t:t + n], s_[0:8, 0:n], ds, reads=[bs])

            for ti in cfg.get('fm', range(25)):
                wt, bw, dw = WR.nxt()
                P.dma("gpsimd", wt[:], wview(w1f, ti * 512, 512), dw, writes=[bw])
                for (t, n) in tts:
                    if ti < 4:
                        for j in range(4):
                            ch = ti * 4 + j
                            bi = fm_group(wt, bw, j * 128, 128, t, n)
                            s_, bs, ds = STB.nxt()
                            sc = (128.0 ** -0.5) if ch < 8 else 1.0
                            X(P, "vector", "tensor_scalar", (s_[:, 0:n], ps_t[:, bi, 0:n], sc, None, ALU.mult),
                              reads=[banks[bi]], writes=[bs])
                            dstT = QT if ch < 8 else KT
                            P.dma("sync", dstT[ch % 8, :, t:t + n], s_[:, 0:n], ds, reads=[bs])
                    elif ti < 9:
                        for jp in range(4):
                            pi = (ti - 4) * 4 + jp
                            b0 = fm_group(wt, bw, jp * 128, 128, t, n)
                            sp, bsp, _ = STB.nxt()
                            X(P, "scalar", "copy", (sp[:, 0:n], ps_t[:, b0, 0:n]), reads=[banks[b0]], writes=[bsp])
                            b1 = nbank(P)
                            MM(P, b1, ps_t[:, b1, 0:n], [(pmb[:, :], sp[:, 0:n])], reads=[B_c, bsp])
                            f0, bf0, _ = STF.nxt()
                            X(P, "vector", "tensor_tensor", (f0[:, 0:n], ps_t[:, b0, 0:n], cst[:, t - h0:t - h0 + n], ALU.mult),
                              reads=[banks[b0], B_tab, bsp], writes=[bf0])
                            f1, bf1, _ = STF.nxt()
                            X(P, "vector", "tensor_tensor", (f1[:, 0:n], ps_t[:, b1, 0:n], snt[:, t - h0:t - h0 + n], ALU.mult),
                              reads=[banks[b1], B_tab], writes=[bf1])
                            s_, bs, ds = STB.nxt()
                            X(P, "vector", "tensor_tensor", (s_[:, 0:n], f0[:, 0:n], f1[:, 0:n], ALU.add),
                              reads=[bf0, bf1], writes=[bs])
                            if pi < 16:
                                P.dma("sync", AQR[pi, :, t:t + n], s_[:, 0:n], ds, reads=[bs])
                            else:
                                kh = pi - 16
                                P.dma("sync", AKR[kh, :, t:t + n], s_[:, 0:n], ds, reads=[bs])
                                oc = None
                                if t == 1536:
                                    oc = (384, 128, pk)
                                elif t == 2048:
                                    oc = (0, 32, sk)
                                if oc is not None:
                                    c_, m_, dst_ = oc
                                    f2, bf2, _ = STF.nxt()
                                    X(P, "vector", "tensor_tensor", (f2[:, 0:m_], f0[:, c_:c_ + m_], f1[:, c_:c_ + m_], ALU.add),
                                      reads=[bf0, bf1], writes=[bf2])
                                    bt = nbank(P)
                                    X(P, "tensor", "transpose", (ps_t[0:128, bt, 0:128], f2[:, 0:128], idf[:, :]),
                                      reads=[bf2, B_c], writes=[banks[bt]])
                                    f3, bf3, df3 = STF.nxt()
                                    X(P, "vector", "tensor_copy", (f3[0:m_, 0:64], ps_t[0:m_, bt, 0:64]),
                                      reads=[banks[bt]], writes=[bf3])
                                    P.dma("sync", dst_[:, kh * 64:(kh + 1) * 64], f3[0:m_, 0:64], df3, reads=[bf3])
                    else:
                        for j in range(4):
                            ch = (ti - 9) * 4 + j
                            bi = fm_group(wt, bw, j * 128, 128, t, n)
                            s_, bs, ds = STB.nxt()
                            X(P, "scalar", "activation", (s_[:, 0:n], ps_t[:, bi, 0:n], AF.Sigmoid),
                              reads=[banks[bi]], writes=[bs])
                            P.dma("sync", GT[ch, :, t:t + n], s_[:, 0:n], ds, reads=[bs])

            for ti in cfg.get('tm', range(9)):
                ncol = 512 if ti < 8 else 256
                wt, bw, dw = WR.nxt()
                P.dma("gpsimd", wt[:, :, 0:ncol], wview(w1t, ti * 512, ncol), dw, writes=[bw])
                for (t, n) in blocks:
                    bi = nbank(P)
                    pairs = [(xT[:, k, t - h0:t - h0 + n], wt[:, k, 0:ncol]) for k in range(32)]
                    MM(P, bi, ps_t[0:n, bi, 0:ncol], pairs, reads=[bw, B_xT])
                    s_, bs, ds = STB.nxt()
                    if 4 <= ti < 8:
                        X(P, "scalar", "activation", (s_[0:n, 0:ncol], ps_t[0:n, bi, 0:ncol], AF.Sigmoid),
                          reads=[banks[bi]], writes=[bs])
                    else:
                        X(P, "vector", "tensor_copy", (s_[0:n, 0:ncol], ps_t[0:n, bi, 0:ncol]),
                          reads=[banks[bi]], writes=[bs])
                    if ti < 4:
                        P.dma("sync", MV[t:t + n, ti * 512:(ti + 1) * 512], s_[0:n, 0:512], ds, reads=[bs])
                    elif ti < 8:
                        P.dma("sync", SIGMO[t:t + n, (ti - 4) * 512:(ti - 3) * 512], s_[0:n, 0:512], ds, reads=[bs])
                    else:
                        P.dma("sync", AV[t:t + n, :], s_[0:n, 0:256], ds, reads=[bs])
                        if t == 1920 or t == 2048:
                            f2, bf2, df2 = STF.nxt()
                            X(P, "scalar", "copy", (f2[0:n, 0:256], ps_t[0:n, bi, 0:256]),
                              reads=[banks[bi], bs], writes=[bf2])
                            P.dma("sync", (pv if t == 1920 else sv)[:, :], f2[0:n, 0:256], df2, reads=[bf2])
        P.run()
        st.close()


    def phase2():
        P = Phase(K, "p2")
        st = ExitStack()
        GW = 2176
        tiles = {}

        def T(name, shape=None, dt=F32):
            t_ = sb(st, "m_" + name, shape or [8, GW], dt)
            tiles[name] = (t_, Buf(name))
            return tiles[name]
        ig, Big = T("ig"); fp, Bfp = T("fp"); l1, Bl1 = T("l1"); nb, Bnb = T("nb"); G, BG = T("G"); mu, Bmu = T("mu")
        wk, Bwk = T("wk"); rr, Brr = T("rr"); aa, Baa = T("aa"); fl, Bfl = T("fl"); ones, Bones = T("ones")
        ME, BME = T("ME", [8, 17]); MS, BMS = T("MS", [8, 17]); nME, BnME = T("nME", [8, 17]); dec, Bdec = T("dec", [8, 17])
        stm_t, Bstm = T("stm", [8, 1]); mo_t, Bmo = T("mo", [8, 2])
        rhsD, BrhsD = T("rhsD", [8, 8, 17])
        idf, Bidf = T("idf", [128, 128]); idb, Bidb = T("idb", [128, 128], BF16)
        msk, Bmsk = T("msk", [128, 128])
        cols, Bcols = T("cols", [128, 17, 32]); decb, Bdecb = T("decb", [128, 8, 17])
        wmnb, Bwmnb = T("wmnb", [128, 2048])
        CTf = sb(st, "CTf", [128, 8, 257], F32)
        CTb = sb(st, "CTb", [128, 8, 257], BF16)
        BCf = [Buf() for _ in range(8)]
        BCb = [Buf() for _ in range(8)]
        sct, Bsct = T("sct", [128, 8, 2, 128])
        npad, Bnpad = T("npad", [128, 128])
        stn_t, Bstn = T("stn", [8, 128])
        QR = Ring(st, "qt", 2, [128, 8, 128], BF16)
        KR = Ring(st, "kt", 2, [128, 8, 128], BF16)
        VR = Ring(st, "vv", 2, [128, 8, 257], BF16)
        SG = Ring(st, "sg", 2, [128, 2048], BF16)
        GWR = Ring(st, "gw", 1, [128, 2048], BF16)
        STM = Ring(st, "stm", 3, [128, 128], BF16)
        KWR = Ring(st, "kw", 3, [128, 128], BF16)
        TMP = Ring(st, "tmp", 2, [128, 257], F32)
        HNR = Ring(st, "hn", 1, [128, 8, 257], F32)
        SM = Ring(st, "sm", 2, [128, 64], F32)
        SQ = Ring(st, "sq", 2, [128, 256], F32)
        YA = Ring(st, "ya", 2, [128, 2048], BF16)
        YAT = Ring(st, "yat", 2, [128, 16, 128], BF16)
        SO = Ring(st, "so", 2, [128, 2, 128], F32)
        D_c = K.new_dsem()

        def V(meth, args, reads=(), writes=(), pwrites=(), **kw):
            return X(P, "vector", meth, args, reads, writes, pwrites, **kw)

        def A(meth, args, reads=(), writes=(), pwrites=(), **kw):
            return X(P, "scalar", meth, args, reads, writes, pwrites, **kw)

        def G_(meth, args, reads=(), writes=(), pwrites=(), **kw):
            return X(P, "gpsimd", meth, args, reads, writes, pwrites, **kw)

        P.dma("sync", ig[:, 0:NTOK], IG[:, :], K.new_dsem(), writes=[Big])
        P.dma("sync", fp[:, 0:NTOK], FP[:, :], K.new_dsem(), writes=[Bfp])
        P.dma("sync", stm_t[:], stm[:, :], K.new_dsem(), writes=[Bstm])
        P.dma("sync", idf[:], identf[:, :], K.new_dsem(), writes=[Bidf])
        P.dma("sync", idb[:], identb[:, :], K.new_dsem(), writes=[Bidb])
        P.dma("sync", msk[:], maskT[:, :], K.new_dsem(), writes=[Bmsk])
        P.dma("sync", wmnb[:], wmn[0, :].partition_broadcast(128), K.new_dsem(), writes=[Bwmnb])
        G_("memset", (ones[:], 1.0), writes=[Bones])
        for (t_, b_) in ((wk, Bwk), (rr, Brr), (aa, Baa), (fl, Bfl)):
            G_("memset", (t_[:], 0.0), writes=[b_])
        G_("memset", (npad[:], 0.0), writes=[Bnpad])
        A("activation", (l1[:, 0:NTOK], fp[:, 0:NTOK], AF.Exp), reads=[Bfp], writes=[Bl1], scale=-1.0)
        A("activation", (l1[:, 0:NTOK], l1[:, 0:NTOK], AF.Ln), reads=[Bl1], writes=[Bl1], bias=1.0)
        Bnb.new_gen()
        V("tensor_tensor_scan", (nb[:, 0:S], ones[:, 0:S], l1[:, 0:S], 0.0, ALU.mult, ALU.add), reads=[Bones, Bl1], pwrites=[Bnb])
        V("tensor_tensor_scan", (nb[:, S:NTOK], ones[:, S:NTOK], l1[:, S:NTOK], 0.0, ALU.mult, ALU.add), reads=[Bones, Bl1], pwrites=[Bnb])
        V("tensor_tensor", (G[:, 0:NTOK], ig[:, 0:NTOK], nb[:, 0:NTOK], ALU.add), reads=[Big, Bnb], writes=[BG])
        Bmu.new_gen()
        V("tensor_tensor_scan", (mu[:, 0:S], ones[:, 0:S], G[:, 0:S], 0.0, ALU.mult, ALU.max), reads=[Bones, BG], pwrites=[Bmu])
        V("tensor_tensor_scan", (mu[:, S:NTOK], ones[:, S:NTOK], G[:, S:NTOK], stm_t[:, 0:1], ALU.mult, ALU.max),
          reads=[Bones, BG, Bstm], pwrites=[Bmu])
        BME.new_gen()
        V("tensor_copy", (ME[:, 0:16], mu[:, 0:S].rearrange("p (c t) -> p c t", t=128)[:, :, 127]), reads=[Bmu], pwrites=[BME])
        V("tensor_copy", (ME[:, 16:17], mu[:, NTOK - 1:NTOK]), reads=[Bmu], pwrites=[BME])
        BMS.new_gen()
        G_("memset", (MS[:, 0:1], 0.0), pwrites=[BMS])
        V("tensor_copy", (MS[:, 1:16], ME[:, 0:15]), reads=[BME], pwrites=[BMS])
        V("tensor_copy", (MS[:, 16:17], stm_t[:, 0:1]), reads=[Bstm], pwrites=[BMS])
        V("tensor_scalar", (nME[:], ME[:], -1.0, None, ALU.mult), reads=[BME], writes=[BnME])
        V("tensor_tensor", (dec[:], MS[:], ME[:], ALU.subtract), reads=[BMS, BME], writes=[Bdec])
        A("activation", (dec[:], dec[:], AF.Exp), reads=[Bdec], writes=[Bdec])
        V("tensor_tensor", (fl[:, 0:NTOK], nb[:, 0:NTOK], mu[:, 0:NTOK], ALU.subtract), reads=[Bnb, Bmu], writes=[Bfl])
        A("activation", (fl[:, 0:NTOK], fl[:, 0:NTOK], AF.Exp), reads=[Bfl], writes=[Bfl])
        Bwk.new_gen(); Brr.new_gen(); Baa.new_gen()
        for c, (t, n) in enumerate(TBLOCKS):
            A("activation", (wk[:, t:t + n], G[:, t:t + n], AF.Exp), reads=[BG, BnME], pwrites=[Bwk], bias=nME[:, c:c + 1])
            A("activation", (rr[:, t:t + n], mu[:, t:t + n], AF.Exp), reads=[Bmu, BME], pwrites=[Brr], bias=ME[:, c:c + 1], scale=-1.0)
            A("activation", (aa[:, t:t + n], mu[:, t:t + n], AF.Exp), reads=[Bmu, BMS], pwrites=[Baa], bias=MS[:, c:c + 1], scale=-1.0)
        Bmo.new_gen()
        V("tensor_tensor", (mo_t[:, 0:1], mu[:, S - 1:S], nb[:, S - 1:S], ALU.subtract), reads=[Bmu, Bnb], pwrites=[Bmo])
        V("tensor_tensor", (mo_t[:, 1:2], mu[:, NTOK - 1:NTOK], nb[:, NTOK - 1:NTOK], ALU.subtract), reads=[Bmu, Bnb], pwrites=[Bmo])
        P.dma("sync", pm[:, :], mo_t[:, 0:1], K.new_dsem(), reads=[Bmo])
        P.dma("sync", sm[:, :], mo_t[:, 1:2], K.new_dsem(), reads=[Bmo])
        Bcols.new_gen()
        for c, (t, n) in enumerate(TBLOCKS):
            bi = nbank(P)
            qs = [(wk, Bwk), (rr, Brr), (aa, Baa), (fl, Bfl)]
            trs = [(ps_t[0:128, bi, q * 8:(q + 1) * 8], qt_[0:8, t:t + 128], idf[0:8, 0:8]) for q, (qt_, _) in enumerate(qs)]

            def fn(e, trs=trs):
                ins = None
                for (o_, i_, d_) in trs:
                    ins = e.transpose(o_, i_, d_)
                return ins
            P.op("tensor", fn, reads=[Bwk, Brr, Baa, Bfl, Bidf], writes=[banks[bi]])
            V("tensor_copy", (cols[:, c, :], ps_t[:, bi, 0:32]), reads=[banks[bi]], pwrites=[Bcols])
        BrhsD.new_gen()
        for h in range(8):
            V("tensor_scalar", (rhsD[:, h, :], dec[:, :], idf[0:8, h:h + 1], None, ALU.mult), reads=[Bdec, Bidf], pwrites=[BrhsD])
        bi = nbank(P)
        MM(P, bi, ps_t[:, bi, 0:136], [(ones[0:8, 0:128], rhsD[:].rearrange("p h c -> p (h c)"))], reads=[Bones, BrhsD])
        V("tensor_copy", (decb[:].rearrange("p h c -> p (h c)"), ps_t[:, bi, 0:136]), reads=[banks[bi]], writes=[Bdecb])
        for i in range(2):
            G_("memset", (VR.t[i][:, :, 256:257], 1.0), writes=[VR.b[i]])
        for h in range(8):
            G_("memset", (CTf[:, h, :], 0.0), writes=[BCf[h]])
            G_("memset", (CTb[:, h, :], 0.0), writes=[BCb[h]])

        def emit_state(dC, dn):
            for h in range(8):
                bi = nbank(P)
                trs = [(ps_t[:, bi, vb * 128:(vb + 1) * 128], CTf[:, h, vb * 128:(vb + 1) * 128], idf[:, :]) for vb in range(2)]

                def fn(e, trs=trs):
                    ins = None
                    for (o_, i_, d_) in trs:
                        ins = e.transpose(o_, i_, d_)
                    return ins
                P.op("tensor", fn, reads=[BCf[h], Bidf], writes=[banks[bi]])
                so, bso, dso = SO.nxt()
                V("tensor_copy", (so[:].rearrange("p a b -> p (a b)"), ps_t[:, bi, 0:256]), reads=[banks[bi]], writes=[bso])
                P.dma("sync", dC[h].rearrange("(vb p) d -> p vb d", p=128), so[:], dso, reads=[bso])
            V("tensor_copy", (npad[:, 0:8], CTf[:, :, 256]), reads=BCf, writes=[Bnpad])
            bi = nbank(P)
            X(P, "tensor", "transpose", (ps_t[:, bi, 0:128], npad[:, :], idf[:, :]), reads=[Bnpad, Bidf], writes=[banks[bi]])
            so, bso, dso = SO.nxt()
            V("tensor_copy", (so[0:8, 0, :], ps_t[0:8, bi, 0:128]), reads=[banks[bi]], writes=[bso])
            P.dma("sync", dn[:, :], so[0:8, 0, :], dso, reads=[bso])

        def load_sample_state():
            P.dma("sync", sct[:], stC.rearrange("h (vb p) d -> p h vb d", p=128), K.new_dsem(), writes=[Bsct])
            P.dma("sync", stn_t[:], stn[:, :], K.new_dsem(), writes=[Bstn])
            for h in range(8):
                bi = nbank(P)
                trs = [(ps_t[:, bi, vb * 128:(vb + 1) * 128], sct[:, h, vb, :], idf[:, :]) for vb in range(2)]

                def fn(e, trs=trs):
                    ins = None
                    for (o_, i_, d_) in trs:
                        ins = e.transpose(o_, i_, d_)
                    return ins
                P.op("tensor", fn, reads=[Bsct, Bidf], writes=[banks[bi]])
                V("tensor_copy", (CTf[:, h, 0:256], ps_t[:, bi, 0:256]), reads=[banks[bi]], writes=[BCf[h]])
            bi = nbank(P)
            X(P, "tensor", "transpose", (ps_t[:, bi, 0:8], stn_t[0:8, :], idf[0:8, 0:8]), reads=[Bstn, Bidf], writes=[banks[bi]])
            V("tensor_copy", (CTf[:, :, 256], ps_t[:, bi, 0:8]), reads=[banks[bi]], pwrites=BCf)
            for h in range(8):
                A("copy", (CTb[:, h, :], CTf[:, h, :]), reads=[BCf[h]], writes=[BCb[h]])

        for c, (t, n) in enumerate(TBLOCKS):
            if c == 16:
                emit_state(pC, pn)
                load_sample_state()
            qt, bq, dq = QR.nxt()
            kt, bk, dk = KR.nxt()
            vv, bv, dv = VR.nxt()
            sg, bsg, dsg = SG.nxt()
            P.dma("sync", qt[:, :, 0:n], QT[:, :, t:t + n].rearrange("h p t -> p h t"), dq, writes=[bq])
            P.dma("sync", kt[:, :, 0:n], KT[:, :, t:t + n].rearrange("h p t -> p h t"), dk, writes=[bk])
            P.dma("sync", vv[0:n, :, 0:256], MV[t:t + n, :].rearrange("t (h v) -> t h v", h=8), dv, writes=[bv])
            P.dma("sync", sg[0:n, :], SIGMO[t:t + n, :], dsg, writes=[bsg])
            gw, bgw, _ = GWR.nxt()
            G_("tensor_tensor", (gw[0:n, :], sg[0:n, :], wmnb[0:n, :], ALU.mult), reads=[bsg, Bwmnb], writes=[bgw])
            hn, bhn, _ = HNR.nxt()
            sm_, bsm, _ = SM.nxt()
            bhn.new_gen()
            bsm.new_gen()
            def part1(h):
                b_s = nbank(P)
                MM(P, b_s, ps_t[0:n, b_s, 0:n], [(kt[:, h, 0:n], qt[:, h, 0:n])], reads=[bk, bq])
                stm_, bstm_, _ = STM.nxt()
                V("scalar_tensor_tensor", (stm_[0:n, 0:n], ps_t[0:n, b_s, 0:n], cols[0:n, c, h:h + 1], msk[0:n, 0:n], ALU.mult, ALU.mult),
                  reads=[banks[b_s], Bcols, Bmsk], writes=[bstm_])
                kw, bkw, _ = KWR.nxt()
                b_t = nbank(P)
                pkb = ps_t[:, b_t, :].bitcast(BF16)
                X(P, "tensor", "transpose", (pkb[0:n, 0:128], kt[:, h, 0:n], idb[:, :]), reads=[bk, Bidb], writes=[banks[b_t]])
                A("activation", (kw[0:n, :], pkb[0:n, 0:128], AF.Copy), reads=[banks[b_t], Bcols], writes=[bkw],
                  scale=cols[0:n, c, h:h + 1])
                return stm_, bstm_, kw, bkw

            def part2(h, stm_, bstm_, kw, bkw):
                b_i = nbank(P)
                MM(P, b_i, ps_t[0:n, b_i, 0:257], [(qt[:, h, 0:n], CTb[:, h, :])], reads=[bq, BCb[h]])
                b_a = nbank(P)
                MM(P, b_a, ps_t[0:n, b_a, 0:257], [(stm_[0:n, 0:n], vv[0:n, h, :])], reads=[bstm_, bv])
                b_d = nbank(P)
                MM(P, b_d, ps_t[:, b_d, 0:257], [(kw[0:n, :], vv[0:n, h, :])], reads=[bkw, bv])
                tmp, btmp, _ = TMP.nxt()
                V("tensor_scalar", (tmp[0:n, :], ps_t[0:n, b_i, 0:257], cols[0:n, c, 16 + h:17 + h], None, ALU.mult),
                  reads=[banks[b_i], Bcols], writes=[btmp])
                V("scalar_tensor_tensor", (hn[0:n, h, :], ps_t[0:n, b_a, 0:257], cols[0:n, c, 8 + h:9 + h], tmp[0:n, :], ALU.mult, ALU.add),
                  reads=[banks[b_a], Bcols, btmp], pwrites=[bhn])
                V("scalar_tensor_tensor", (CTf[:, h, :], CTf[:, h, :], decb[:, h, c:c + 1], ps_t[:, b_d, 0:257], ALU.mult, ALU.add),
                  reads=[banks[b_d], Bdecb, BCf[h]], writes=[BCf[h]])
                A("copy", (CTb[:, h, :], CTf[:, h, :]), reads=[BCf[h]], writes=[BCb[h]])
                sq, bsq, _ = SQ.nxt()
                A("activation", (sq[0:n, :], hn[0:n, h, 0:256], AF.Square), reads=[bhn], writes=[bsq], pwrites=[bsm],
                  accum_out=sm_[0:n, h:h + 1])

            pend = part1(0)
            for h in range(8):
                nxt_ = part1(h + 1) if h + 1 < 8 else None
                part2(h, *pend)
                pend = nxt_
            dd = sm_[0:n, 8:16]
            tt_ = sm_[0:n, 16:24]
            ff = sm_[0:n, 24:32]
            ss = sm_[0:n, 0:8]
            V("tensor_scalar", (tt_, hn[0:n, :, 256], -1.0, None, ALU.mult), reads=[bhn], pwrites=[bsm])
            V("tensor_tensor", (dd, tt_, hn[0:n, :, 256], ALU.max), reads=[bhn, bsm], writes=[bsm])
            V("tensor_tensor", (dd, dd, cols[0:n, c, 24:32], ALU.max), reads=[bsm, Bcols], writes=[bsm])
            V("reciprocal", (dd, dd), reads=[bsm], writes=[bsm])
            V("tensor_tensor", (tt_, ss, dd, ALU.mult), reads=[bsm], writes=[bsm])
            V("tensor_tensor", (tt_, tt_, dd, ALU.mult), reads=[bsm], writes=[bsm])
            V("tensor_scalar", (tt_, tt_, 1.0 / 256.0, RMS_EPS, ALU.mult, ALU.add), reads=[bsm], writes=[bsm])
            A("activation", (tt_, tt_, AF.Sqrt), reads=[bsm], writes=[bsm])
            V("reciprocal", (tt_, tt_), reads=[bsm], writes=[bsm])
            V("tensor_tensor", (ff, tt_, dd, ALU.mult), reads=[bsm], writes=[bsm])
            ya, bya, _ = YA.nxt()
            bya.new_gen()
            for h in range(8):
                V("scalar_tensor_tensor", (ya[0:n, h * 256:(h + 1) * 256], hn[0:n, h, 0:256], sm_[0:n, 24 + h:25 + h],
                                           gw[0:n, h * 256:(h + 1) * 256], ALU.mult, ALU.mult),
                  reads=[bhn, bsm, bgw], pwrites=[bya])
            yat, byat, dyat = YAT.nxt()
            byat.new_gen()
            for j0 in (0, 8):
                bi = nbank(P)
                psb = ps_t[:, bi, :].bitcast(BF16)
                trs = [(psb[:, j * 128:j * 128 + n], ya[0:n, (j0 + j) * 128:(j0 + j + 1) * 128], idb[0:n, 0:n]) for j in range(8)]

                def fn(e, trs=trs):
                    ins = None
                    for (o_, i_, d_) in trs:
                        ins = e.transpose(o_, i_, d_)
                    return ins
                P.op("tensor", fn, reads=[bya, Bidb], writes=[banks[bi]])
                A("copy", (yat[:, j0:j0 + 8, 0:n], psb[:, 0:1024].rearrange("p (j t) -> p j t", t=128)[:, :, 0:n]),
                  reads=[banks[bi]], pwrites=[byat])
            P.dma("sync", YT[0:16, :, t:t + n].rearrange("c p t -> p c t"), yat[:, :, 0:n], dyat, reads=[byat])
        emit_state(sC, sn)
        P.run()
        st.close()


    def phase3():
        P = Phase(K, "p3")
        st = ExitStack()
        AQ = Ring(st, "aq", 2, [128, 16, 128], BF16)
        AK = Ring(st, "ak", 3, [128, 4, 128], BF16)
        VA = Ring(st, "va", 3, [128, 4, 65], BF16)
        PT = Ring(st, "pt", 8, [128, 4, 128], BF16)
        YB = Ring(st, "yb", 2, [128, 2048], BF16)
        YBT = Ring(st, "ybt", 2, [128, 16, 128], BF16)
        DN = Ring(st, "dn", 4, [128, 8], F32)
        esb = sb(st, "esb", [128, 32], F32); Besb = Buf()
        idf = sb(st, "idf3", [128, 128], F32); Bidf = Buf()
        idb = sb(st, "idb3", [128, 128], BF16); Bidb = Buf()
        ckd = sb(st, "ckd", [128, 4, 2, 64], F32); Bckd = Buf()
        akc = sb(st, "akc", [128, 4, 128], BF16); Bakc = Buf()
        vac = sb(st, "vac", [128, 4, 65], BF16); Bvac = Buf()
        D_c = K.new_dsem()

        def V(meth, args, reads=(), writes=(), pwrites=(), **kw):
            return X(P, "vector", meth, args, reads, writes, pwrites, **kw)

        def A(meth, args, reads=(), writes=(), pwrites=(), **kw):
            return X(P, "scalar", meth, args, reads, writes, pwrites, **kw)

        def G_(meth, args, reads=(), writes=(), pwrites=(), **kw):
            return X(P, "gpsimd", meth, args, reads, writes, pwrites, **kw)

        P.dma("sync", esb[:], sinkr[0, :].partition_broadcast(128), K.new_dsem(), writes=[Besb])
        A("activation", (esb[:], esb[:], AF.Exp), reads=[Besb], writes=[Besb])
        P.dma("sync", idf[:], identf[:, :], K.new_dsem(), writes=[Bidf])
        P.dma("sync", idb[:], identb[:, :], K.new_dsem(), writes=[Bidb])
        for i in range(3):
            G_("memset", (VA.t[i][:, :, 64:65], 1.0), writes=[VA.b[i]])
        G_("memset", (vac[:, :, 64:65], 1.0), writes=[Bvac])
        Bckd.new_gen()
        ckv = ck.rearrange("j (k d) -> j k d", k=4)
        P.dma("sync", ckd[:, :, 0, :], ckv, K.new_dsem(), pwrites=[Bckd])
        P.dma("sync", ckd[:, :, 1, :], ckv, K.new_dsem(), pwrites=[Bckd])
        P.dma("gpsimd", vac[:, :, 0:64], cv.rearrange("j (k d) -> j k d", k=4), K.new_dsem(), reads=[], writes=[Bvac])
        bi = nbank(P)
        trs = [(ps_t[:, bi, kh * 128:(kh + 1) * 128], ckd[:, kh, :, :].rearrange("p a d -> p (a d)"), idf[:, :]) for kh in range(4)]

        def fnc(e, trs=trs):
            ins = None
            for (o_, i_, d_) in trs:
                ins = e.transpose(o_, i_, d_)
            return ins
        P.op("tensor", fnc, reads=[Bckd, Bidf], writes=[banks[bi]])
        V("tensor_copy", (akc[:].rearrange("p k j -> p (k j)"), ps_t[:, bi, 0:512]), reads=[banks[bi]], writes=[Bakc])

        prev = None
        for c, (t, n) in enumerate(TBLOCKS):
            aq, baq, daq = AQ.nxt()
            P.dma("sync", aq[:, :, 0:n], AQR[:, :, t:t + n].rearrange("c p t -> p c t"), daq, writes=[baq])
            ak, bak, dak = AK.nxt()
            P.dma("sync", ak[:, :, 0:n], AKR[:, :, t:t + n].rearrange("c p t -> p c t"), dak, writes=[bak])
            va, bva, dva = VA.nxt()
            P.dma("sync", va[0:n, :, 0:64], AV[t:t + n, :].rearrange("t (k d) -> t k d", k=4), dva, writes=[bva])
            if c == 16:
                ktiles = [(akc, Bakc, vac, Bvac, 128, None), (ak, bak, va, bva, 128, "new")]
            elif c == 0:
                ktiles = [(ak, bak, va, bva, 128, "cur")]
            else:
                ktiles = [prev + (128, "prev"), (ak, bak, va, bva, 128, "cur")]
            prev = (ak, bak, va, bva)
            yb, byb, _ = YB.nxt()
            byb.new_gen()
            for kvh in range(4):
                pts = {}
                for ki, (akt, bakt, vat, bvat, nj, role) in enumerate(ktiles):
                    for par in range(2):
                        bs_ = nbank(P)
                        o3 = ps_t[0:nj, bs_, 0:4 * n].rearrange("p (a q) -> p a q", a=4)
                        MM(P, bs_, o3, [(akt[par * 64:(par + 1) * 64, kvh, 0:nj],
                                         aq[par * 64:(par + 1) * 64, kvh * 4:(kvh + 1) * 4, 0:n])], reads=[bakt, baq])
                        pt, bpt, _ = PT.nxt()
                        A("activation", (pt[0:nj, :, 0:n], o3, AF.Exp), reads=[banks[bs_]], writes=[bpt], scale=0.125)
                        if role == "prev":
                            G_("memset", (pt[0:64, :, 64:128], 0.0), reads=[bpt], pwrites=[bpt])
                        elif role == "new":
                            G_("memset", (pt[32:64, :, 0:n], 0.0), reads=[bpt], pwrites=[bpt])
                            G_("memset", (pt[64:128, :, 0:n], 0.0), reads=[bpt], pwrites=[bpt])
                        elif role == "cur":
                            G_("memset", (pt[64:128, :, 0:64], 0.0), reads=[bpt], pwrites=[bpt])
                        pts[(ki, par)] = (pt, bpt)
                for par in range(2):
                    bo = nbank(P)
                    mms = []
                    rds = []
                    for pair in range(4):
                        nk = len(ktiles)
                        for ki, (akt, bakt, vat, bvat, nj, role) in enumerate(ktiles):
                            pt, bpt = pts[(ki, par)]
                            mms.append((ps_t[0:n, bo, pair * 65:(pair + 1) * 65], pt[0:nj, pair, 0:n], vat[0:nj, kvh, :],
                                        ki == 0, ki == nk - 1))
                            rds += [bpt, bvat]

                    def fnv(e, mms=mms):
                        ins = None
                        for (o_, l_, r_, s0, s1) in mms:
                            ins = e.matmul(o_, l_, r_, start=s0, stop=s1)
                        return ins
                    P.op("tensor", fnv, reads=rds, writes=[banks[bo]])
                    dn, bdn, _ = DN.nxt()
                    hb = kvh * 8 + par * 4
                    V("tensor_tensor", (dn[0:n, 0:4], ps_t[0:n, bo, 0:260].rearrange("p (a c) -> p a c", c=65)[:, :, 64],
                                        esb[0:n, hb:hb + 4], ALU.add), reads=[banks[bo], Besb], writes=[bdn])
                    V("reciprocal", (dn[0:n, 0:4], dn[0:n, 0:4]), reads=[bdn], writes=[bdn])
                    for pair in range(4):
                        h = kvh * 8 + pair * 2 + par
                        V("tensor_scalar", (yb[0:n, h * 64:(h + 1) * 64], ps_t[0:n, bo, pair * 65:pair * 65 + 64],
                                            dn[0:n, pair:pair + 1], None, ALU.mult), reads=[banks[bo], bdn], pwrites=[byb])
            ybt, bybt, dybt = YBT.nxt()
            bybt.new_gen()
            for j0 in (0, 8):
                bi = nbank(P)
                psb = ps_t[:, bi, :].bitcast(BF16)
                trs = [(psb[:, j * 128:j * 128 + n], yb[0:n, (j0 + j) * 128:(j0 + j + 1) * 128], idb[0:n, 0:n]) for j in range(8)]

                def fn(e, trs=trs):
                    ins = None
                    for (o_, i_, d_) in trs:
                        ins = e.transpose(o_, i_, d_)
                    return ins
                P.op("tensor", fn, reads=[byb, Bidb], writes=[banks[bi]])
                A("copy", (ybt[:, j0:j0 + 8, 0:n], psb[:, 0:1024].rearrange("p (j t) -> p j t", t=128)[:, :, 0:n]),
                  reads=[banks[bi]], pwrites=[bybt])
            P.dma("sync", YT[16:32, :, t:t + n].rearrange("c p t -> p c t"), ybt[:, :, 0:n], dybt, reads=[bybt])
        P.run()
        st.close()


    P4_GROUPS = ((0, 768), (768, 768), (1536, 544))

    def phase4():
        P = Phase(K, "p4")
        st = ExitStack()
        YTs = sb(st, "yts", [128, 32, 768], BF16); B_yts = Buf()
        MT = sb(st, "mt", [128, 32, 768], BF16); B_mt = Buf()
        WR = Ring(st, "wr4", 2, [128, 32, 512], BF16)
        GTR = Ring(st, "gtr", 2, [128, 2, 4, 512], BF16)
        XSR = Ring(st, "xs4", 3, [128, 512], F32)
        STF = Ring(st, "stf4", 4, [128, 512], F32)
        B_yp = [Buf() for _ in range(2)]
        for (h0, hn) in cfg.get('p4g', P4_GROUPS):
            tts = ttiles(h0, hn)
            for i_, (t, n) in enumerate(tts):
                P.dma("sync", YTs[:, :, t - h0:t - h0 + n], YT[:, :, t:t + n].rearrange("c p t -> p c t"), K.new_dsem(),
                      writes=[B_yp[i_]])
            B_mt.new_gen()
            for et in range(8):
                wt, bw, dw = WR.nxt()
                P.dma("gpsimd", wt[:, 0:16, :], wview(wa, et * 512, 512, 0, 16), dw, writes=[bw])
                P.dma("gpsimd", wt[:, 16:32, :], wview(wb, et * 512, 512, 0, 16), dw, pwrites=[bw])
                for i_, (t, n) in enumerate(tts):
                    gt, bg_, dg = GTR.nxt()
                    P.dma("sync", gt[:, 0, :, 0:n], GT[et * 4:(et + 1) * 4, :, t:t + n].rearrange("c p t -> p c t"), dg, writes=[bg_])
                    P.dma("sync", gt[:, 1, :, 0:n], GT[32 + et * 4:32 + (et + 1) * 4, :, t:t + n].rearrange("c p t -> p c t"), dg, pwrites=[bg_])
                    for j in range(4):
                        ch = et * 4 + j
                        ba = nbank(P)
                        MM(P, ba, ps_t[:, ba, 0:n], [(wt[:, k, j * 128:(j + 1) * 128], YTs[:, k, t - h0:t - h0 + n]) for k in range(16)],
                           reads=[bw, B_yp[i_]])
                        bb = nbank(P)
                        MM(P, bb, ps_t[:, bb, 0:n], [(wt[:, k, j * 128:(j + 1) * 128], YTs[:, k, t - h0:t - h0 + n]) for k in range(16, 32)],
                           reads=[bw, B_yp[i_]])
                        f0, bf0, _ = STF.nxt()
                        X(P, "vector", "tensor_tensor", (f0[:, 0:n], ps_t[:, ba, 0:n], gt[:, 0, j, 0:n], ALU.mult),
                          reads=[banks[ba], bg_], writes=[bf0])
                        f1, bf1, _ = STF.nxt()
                        X(P, "vector", "tensor_tensor", (f1[:, 0:n], ps_t[:, bb, 0:n], gt[:, 1, j, 0:n], ALU.mult),
                          reads=[banks[bb], bg_], writes=[bf1])
                        X(P, "vector", "tensor_tensor", (MT[:, ch, t - h0:t - h0 + n], f0[:, 0:n], f1[:, 0:n], ALU.add),
                          reads=[bf0, bf1], pwrites=[B_mt])
            blocks = [(t, n) for (t, n) in TBLOCKS if h0 <= t < h0 + hn]
            for ct in range(8):
                wt, bw, dw = WR.nxt()
                P.dma("gpsimd", wt[:], wview(wo, ct * 512, 512), dw, writes=[bw])
                for (t, n) in blocks:
                    xs, bxs, dxs = XSR.nxt()
                    P.dma("sync", xs[0:n, :], x[t:t + n, ct * 512:(ct + 1) * 512], dxs, writes=[bxs])
                    bi = nbank(P)
                    MM(P, bi, ps_t[0:n, bi, 0:512], [(MT[:, k, t - h0:t - h0 + n], wt[:, k, :]) for k in range(32)], reads=[bw, B_mt])
                    f, bf, df = STF.nxt()
                    X(P, "vector", "scalar_tensor_tensor", (f[0:n, :], xs[0:n, :], ALPHA, ps_t[0:n, bi, 0:512], ALU.mult, ALU.add),
                      reads=[bxs, banks[bi]], writes=[bf])
                    P.dma("scalar", HPRE[t:t + n, ct * 512:(ct + 1) * 512], f[0:n, :], df, reads=[bf])
        P.run()
        st.close()

    def ln_phase(name, src_d, gi, dst_d, do_T):
        P = Phase(K, name)
        st = ExitStack()
        lng = sb(st, name + "g", [128, D], F32); Bg = Buf()
        lnb = sb(st, name + "b", [128, D], F32); Bb = Buf()
        idf = sb(st, name + "id", [128, 128], F32); Bidf = Buf()
        HP = Ring(st, name + "hp", 2, [128, D], F32)
        HN = Ring(st, name + "hn", 2, [128, D], F32)
        SS = Ring(st, name + "ss", 2, [128, 8], F32)
        if do_T:
            HTS = Ring(st, name + "ht", 2, [128, 32, 128], BF16)
        P.dma("sync", lng[:], lnp[gi, :].partition_broadcast(128), K.new_dsem(), writes=[Bg])
        P.dma("sync", lnb[:], lnp[gi + 1, :].partition_broadcast(128), K.new_dsem(), writes=[Bb])
        P.dma("sync", idf[:], identf[:, :], K.new_dsem(), writes=[Bidf])
        for (t, n) in cfg.get('lnblocks', TBLOCKS):
            hp, bhp, dhp = HP.nxt()
            P.dma("sync", hp[0:n, :], src_d[t:t + n, :], dhp, writes=[bhp])
            hn, bhn, dhn = HN.nxt()
            ss, bss, _ = SS.nxt()
            bss.new_gen()
            X(P, "scalar", "activation", (hn[0:n, :], hp[0:n, :], AF.Identity), reads=[bhp], writes=[bhn], pwrites=[bss],
              accum_out=ss[0:n, 0:1])
            X(P, "scalar", "activation", (hn[0:n, :], hp[0:n, :], AF.Square), reads=[bhp], writes=[bhn], pwrites=[bss],
              accum_out=ss[0:n, 1:2])
            mean = ss[0:n, 2:3]
            msq = ss[0:n, 3:4]
            var = ss[0:n, 4:5]
            X(P, "vector", "tensor_scalar", (mean, ss[0:n, 0:1], 1.0 / D, None, ALU.mult), reads=[bss], writes=[bss])
            X(P, "vector", "tensor_tensor", (msq, mean, mean, ALU.mult), reads=[bss], writes=[bss])
            X(P, "vector", "scalar_tensor_tensor", (var, ss[0:n, 1:2], 1.0 / D, msq, ALU.mult, ALU.subtract), reads=[bss], writes=[bss])
            X(P, "vector", "tensor_scalar", (var, var, LN_EPS, None, ALU.add), reads=[bss], writes=[bss])
            X(P, "scalar", "activation", (var, var, AF.Sqrt), reads=[bss], writes=[bss])
            X(P, "vector", "reciprocal", (var, var), reads=[bss], writes=[bss])
            X(P, "vector", "scalar_tensor_tensor", (hn[0:n, :], hp[0:n, :], mean, lng[0:n, :], ALU.subtract, ALU.mult),
              reads=[bhp, bss, bhn, Bg], writes=[bhn])
            X(P, "vector", "scalar_tensor_tensor", (hn[0:n, :], hn[0:n, :], var, lnb[0:n, :], ALU.mult, ALU.add),
              reads=[bhn, bss, Bb], writes=[bhn])
            P.dma("scalar", dst_d[t:t + n, :], hn[0:n, :], dhn, reads=[bhn])
            if do_T:
                hts, bhts, dhts = HTS.nxt()
                bhts.new_gen()
                for c0 in range(0, 32, 4):
                    bi = nbank(P)
                    trs = [(ps_t[:, bi, j * n:(j + 1) * n], hn[0:n, (c0 + j) * 128:(c0 + j + 1) * 128], idf[0:n, 0:n]) for j in range(4)]

                    def fn(e, trs=trs):
                        ins = None
                        for (o_, i_, d_) in trs:
                            ins = e.transpose(o_, i_, d_)
                        return ins
                    P.op("tensor", fn, reads=[bhn, Bidf], writes=[banks[bi]])
                    src_ = ps_t[:, bi, 0:4 * n].rearrange("p (j n) -> p j n", j=4)
                    dst_ = hts[:, c0:c0 + 4, 0:n]
                    if (c0 // 4) % 2 == 0:
                        X(P, "vector", "tensor_copy", (dst_, src_), reads=[banks[bi]], pwrites=[bhts])
                    else:
                        X(P, "scalar", "copy", (dst_, src_), reads=[banks[bi]], pwrites=[bhts])
                P.dma("scalar", HT[:, :, t:t + n].rearrange("c p t -> p c t"), hts[:, :, 0:n], dhts, reads=[bhts])
        P.run()
        st.close()

    def phase6():
        P = Phase(K, "p6")
        st = ExitStack()
        hTs = sb(st, "hTs", [128, 32, 1056], BF16); B_h = Buf()
        WR = Ring(st, "wr6", 2, [128, 32, 512], BF16)
        STF = Ring(st, "stf6", 4, [128, 512], F32)
        STB = Ring(st, "stb6", 4, [128, 512], BF16)
        B_hp = [Buf() for _ in range(3)]
        for (h0, hn) in cfg.get('halves', HALVES):
            tts = ttiles(h0, hn)
            for i_, (t, n) in enumerate(tts):
                P.dma("sync", hTs[:, :, t - h0:t - h0 + n], HT[:, :, t:t + n].rearrange("c p t -> p c t"), K.new_dsem(),
                      writes=[B_hp[i_]])
            for ft in cfg.get('p6ft', range(32)):
                wt, bw, dw = WR.nxt()
                P.dma("gpsimd", wt[:], wview(wu, ft * 512, 512), dw, writes=[bw])
                for i_, (t, n) in enumerate(tts):
                    for j in range(4):
                        ch = ft * 4 + j
                        bi = nbank(P)
                        MM(P, bi, ps_t[:, bi, 0:n], [(wt[:, k, j * 128:(j + 1) * 128], hTs[:, k, t - h0:t - h0 + n]) for k in range(32)],
                           reads=[bw, B_hp[i_]])
                        f, bf, _ = STF.nxt()
                        X(P, "scalar", "activation", (f[:, 0:n], ps_t[:, bi, 0:n], AF.Relu), reads=[banks[bi]], writes=[bf])
                        s_, bs, ds = STB.nxt()
                        X(P, "vector", "tensor_tensor", (s_[:, 0:n], f[:, 0:n], f[:, 0:n], ALU.mult), reads=[bf], writes=[bs])
                        P.dma("sync", UT[t // 512, :, ch, 0:n], s_[:, 0:n], ds, reads=[bs])
        P.run()
        st.close()

    def phase7():
        P = Phase(K, "p7")
        st = ExitStack()
        acc = sb(st, "acc", [128, 17, 512], F32)
        Bacc = [Buf() for _ in range(17)]
        WR = Ring(st, "wr7", 2, [128, 32, 512], BF16)
        UTR = Ring(st, "utr", 3, [128, 32, 512], BF16)
        HHR = Ring(st, "hhr", 2, [128, 512], F32)
        OUT = Ring(st, "out7", 3, [128, 512], F32)
        alltiles = ttiles(0, NTOK)
        for cb in cfg.get('p7cb', range(8)):
            for kg in range(4):
                wt, bw, dw = WR.nxt()
                P.dma("gpsimd", wt[:], wview(wd, cb * 512, 512, kg * 32, 32), dw, writes=[bw])
                for (t, n) in alltiles:
                    ut, but, dut = UTR.nxt()
                    P.dma("sync", ut[:, :, 0:n], UT[t // 512, :, kg * 32:(kg + 1) * 32, 0:n], dut, writes=[but])
                    for tb in range(t, t + n, 128):
                        nb_ = min(128, t + n - tb)
                        bx = tb // 128
                        bi = nbank(P)
                        MM(P, bi, ps_t[0:nb_, bi, 0:512], [(ut[:, k, tb - t:tb - t + nb_], wt[:, k, :]) for k in range(32)], reads=[bw, but])
                        if kg == 0:
                            X(P, "scalar", "copy", (acc[0:nb_, bx, :], ps_t[0:nb_, bi, 0:512]), reads=[banks[bi]], writes=[Bacc[bx]])
                        elif kg < 3:
                            X(P, "vector", "tensor_tensor", (acc[0:nb_, bx, :], acc[0:nb_, bx, :], ps_t[0:nb_, bi, 0:512], ALU.add),
                              reads=[banks[bi], Bacc[bx]], writes=[Bacc[bx]])
                        else:
                            hh, bhh, dhh = HHR.nxt()
                            P.dma("sync", hh[0:nb_, :], HH[tb:tb + nb_, cb * 512:(cb + 1) * 512], dhh, writes=[bhh])
                            o_, bo_, do_ = OUT.nxt()
                            X(P, "vector", "tensor_tensor", (o_[0:nb_, :], acc[0:nb_, bx, :], ps_t[0:nb_, bi, 0:512], ALU.add),
                              reads=[banks[bi], Bacc[bx]], writes=[bo_])
                            X(P, "vector", "scalar_tensor_tensor", (o_[0:nb_, :], hh[0:nb_, :], ALPHA, o_[0:nb_, :], ALU.mult, ALU.add),
                              reads=[bhh, bo_], writes=[bo_])
                            P.dma("scalar", FPRE[tb:tb + nb_, cb * 512:(cb + 1) * 512], o_[0:nb_, :], do_, reads=[bo_])
        P.run()
        st.close()

    if cfg.get('p1', True):
        phase1()
    if cfg.get('p2', True):
        phase2()
    if cfg.get('p3', True):
        phase3()
    if cfg.get('p4', True):
        phase4()
    if cfg.get('p5', True):
        ln_phase("p5", HPRE, 0, HH, True)
    if cfg.get('p6', True):
        phase6()
    if cfg.get('p7', True):
        phase7()
    if cfg.get('p8', True):
        ln_phase("p8", FPRE, 2, y, False)
    return nc, stack


def _host_prep(inp):
    w_in = inp["w_in"][0]
    o_mq, o_mk, o_mv, o_mo, o_mi, o_mf = 0, 1024, 2048, 4096, 6144, 6152
    o_aq, o_ak, o_av, o_gp = 6160, 8208, 8464, 8720
    cols = []
    cols += list(range(o_mq, o_mq + 1024))
    cols += list(range(o_mk, o_mk + 1024))
    cols += list(range(o_aq, o_aq + 2048))
    for kh in range(4):
        base = o_ak + kh * 64
        cols += list(range(base, base + 64)) * 2
    cols += list(range(o_gp, o_gp + 8192))
    cols = np.asarray(cols)
    assert cols.size == 12800
    w1f = np.ascontiguousarray(w_in[:, cols])
    wg = np.ascontiguousarray(w_in[:, o_mi:o_mi + 16])
    tcols = np.concatenate([np.arange(o_mv, o_mv + 2048), np.arange(o_mo, o_mo + 2048), np.arange(o_av, o_av + 256)])
    w1t = np.ascontiguousarray(w_in[:, tcols])
    half = 32
    inv = (np.float32(10000.0) ** (-(np.arange(half, dtype=np.float32) / np.float32(half)))).astype(np.float32)
    pos = np.concatenate([np.arange(S), 1024 + np.arange(TS)]).astype(np.float32)
    ang = (pos[None, :] * inv[:, None]).astype(np.float32)
    cos = np.cos(ang.astype(np.float64)).astype(np.float32)
    sin = np.sin(ang.astype(np.float64)).astype(np.float32)
    cos64 = np.concatenate([cos, cos], 0)
    sin64 = np.concatenate([-sin, sin], 0)
    cosT = np.ascontiguousarray(np.concatenate([cos64, cos64], 0))
    sinT = np.ascontiguousarray(np.concatenate([sin64, sin64], 0))
    ident = np.eye(128, dtype=np.float32)
    swap_idx = np.arange(128).reshape(2, 2, 32)[:, ::-1, :].reshape(128)
    permb = np.ascontiguousarray(ident[swap_idx].astype(ml_dtypes.bfloat16))
    maskT = np.triu(np.ones((128, 128), np.float32))
    shared = dict(
        w1f=w1f, wg=wg, w1t=w1t,
        bigf=np.ascontiguousarray(np.stack([inp["b_igate"][0], inp["b_fgate"][0]], 1)),
        wmn=inp["w_mnorm"].reshape(1, 2048),
        wa=inp["w_branch_a"][0], wb=inp["w_branch_b"][0], wo=inp["w_out"][0], wu=inp["w_up"][0], wd=inp["w_down"][0],
        lnp=np.ascontiguousarray(np.stack([inp["ln1_g"][0], inp["ln1_b"][0], inp["ln2_g"][0], inp["ln2_b"][0]], 0)),
        sinkr=np.ascontiguousarray(inp["attn_sink"].reshape(4, 4, 2).transpose(0, 2, 1).reshape(1, 32)),
        identf=ident, identb=ident.astype(ml_dtypes.bfloat16), permb=permb, maskT=maskT, cosT=cosT, sinT=sinT,
    )
    maps = []
    for b in range(8):
        m = dict(shared)
        m["x"] = np.ascontiguousarray(np.concatenate([inp["x_prompt"][b], inp["x_sample"][b]], 0))
        m["ck"] = np.ascontiguousarray(inp["cache_swa_k"][0, b].reshape(128, 256))
        m["cv"] = np.ascontiguousarray(inp["cache_swa_v"][0, b].reshape(128, 256))
        m["stC"] = np.ascontiguousarray(inp["state_mlstm_C"][0, b])
        m["stn"] = np.ascontiguousarray(inp["state_mlstm_n"][0, b])
        m["stm"] = np.ascontiguousarray(inp["state_mlstm_m"][0, b].reshape(8, 1))
        maps.append(m)
    return maps


def kernel(**inputs):
    inp = {k: np.asarray(v) for k, v in inputs.items() if not k.startswith('_')}
    maps = _host_prep(inp)
    dbg = bool(os.environ.get("MK_DEBUG"))
    nc, stack = build(debug=dbg, cfg=inputs.get("_cfg") if dbg else None)
    if dbg:
        res = run_bass_kernel_spmd(nc, maps[:1], core_ids=[0])
    else:
        res = run_bass_kernel_spmd(nc, maps, core_ids=list(range(8)))
    stack.close()
    R = res.results
    if os.environ.get("MK_DEBUG"):
        return R
    g = lambda k: np.stack([np.asarray(r[k]) for r in R], 0)
    yy = g("y")
    outs = (
        yy[:, :S, :], yy[:, S:, :],
        g("pk").reshape(1, 8, 128, 4, 64), g("pv").reshape(1, 8, 128, 4, 64),
        g("pC").reshape(1, 8, 8, 256, 128), g("pn").reshape(1, 8, 8, 128), g("pm").reshape(1, 8, 8),
        g("sk").reshape(1, 8, 32, 4, 64), g("sv").reshape(1, 8, 32, 4, 64),
        g("sC").reshape(1, 8, 8, 256, 128), g("sn").reshape(1, 8, 8, 128), g("sm").reshape(1, 8, 8),
    )
    return tuple(np.ascontiguousarray(o.astype(np.float32)) for o in outs)
```

```python
import os
from contextlib import ExitStack
import numpy as np
import ml_dtypes
import concourse.bass as bass
import concourse.mybir as mybir
from concourse.bass_utils import run_bass_kernel_spmd

F32 = mybir.dt.float32
BF16 = mybir.dt.bfloat16
ALU = mybir.AluOpType
AF = mybir.ActivationFunctionType
AX = mybir.AxisListType

NTOK = 2080
S = 2048
TS = 32
D = 4096
DFF = 16384
ALPHA = 2.0 ** 0.25
LN_EPS = 1e-5
RMS_EPS = 1e-6
ENGS = ("sync", "scalar", "gpsimd", "vector", "tensor")
HALVES = ((0, 1024), (1024, 1056))
TBLOCKS = [(i * 128, 128) for i in range(16)] + [(2048, 32)]


def ttiles(t0, n):
    out = []
    t = t0
    while t < t0 + n:
        w = min(512, t0 + n - t)
        out.append((t, w))
        t += w
    return out


class Ev:
    __slots__ = ("sem", "val", "needed", "dma")

    def __init__(self):
        self.sem = None
        self.val = 0
        self.needed = False
        self.dma = False


class Buf:
    def __init__(self, name=""):
        self.name = name
        self.w = []
        self.r = []
        self.prev = []

    def new_gen(self):
        self.prev = list(self.r) if self.r else list(self.w)
        self.w = []
        self.r = []


class DSem:
    def __init__(self, K, name):
        self.K = K
        self.name = name
        self._sem = None
        self.count = 0

    @property
    def sem(self):
        if self._sem is None:
            self._sem = self.K.new_sem(self.name)
        return self._sem


class Op:
    __slots__ = ("fn", "waits", "ev")


class Kern:
    def __init__(self, nc, stack):
        self.nc = nc
        self.stack = stack
        self.nsem = 0
        self.phase_idx = 0
        self.pool = []
        self.bar = self.new_sem("bar")
        self.dsems = []
        self.esem = {e: self.new_sem(f"eng_{e}") for e in ENGS if e != "sync"}
        self.ecount = {e: 0 for e in ENGS}

    def new_sem(self, name):
        if self.pool:
            return self.pool.pop()
        self.nsem += 1
        return self.stack.enter_context(self.nc.semaphore(f"{name}_{self.nsem}"))

    def new_dsem(self, name="d"):
        d = DSem(self, name)
        self.dsems.append(d)
        return d


class Phase:
    def __init__(self, K, name):
        self.K = K
        self.nc = K.nc
        self.name = name
        self.q = {e: [] for e in ENGS}
        self.esem = K.esem
        self.bank_i = 0

    def op(self, eng, fn, reads=(), writes=(), pwrites=(), dsem=None):
        o = Op()
        o.fn = fn
        ev = Ev()
        waits = []
        for b in reads:
            waits += b.w
        for b in writes:
            b.new_gen()
            waits += b.prev
        for b in pwrites:
            waits += b.prev
        seen = set()
        o.waits = []
        for w in waits:
            if id(w) not in seen:
                seen.add(id(w))
                w.needed = True
                o.waits.append(w)
        if dsem is not None:
            dsem.count += 16
            ev.sem = dsem.sem
            ev.val = dsem.count
            ev.dma = True
            ev.needed = True
        o.ev = ev
        for b in reads:
            b.r.append(ev)
        for b in writes:
            b.w.append(ev)
        for b in pwrites:
            b.w.append(ev)
        self.q[eng].append(o)
        return ev

    def dma(self, eng, out, in_, dsem, reads=(), writes=(), pwrites=()):
        return self.op(eng, lambda e: e.dma_start(out=out, in_=in_), reads, writes, pwrites, dsem)

    def run(self):
        K = self.K
        nc = self.nc
        for eng in ENGS:
            if eng == "sync":
                continue
            q = self.q[eng]
            last = None
            for o in q:
                if not o.ev.dma:
                    last = o
            if last is not None:
                last.ev.needed = True
            cnt = K.ecount[eng]
            for o in q:
                if o.ev.dma:
                    continue
                if o.ev.needed:
                    cnt += 1
                    o.ev.sem = self.esem[eng]
                    o.ev.val = cnt
            K.ecount[eng] = cnt
            self._final = getattr(self, "_final", {})
            self._final[eng] = (self.esem[eng], cnt)
        pidx = K.phase_idx
        finals = [(s, v) for (s, v) in self._final.values() if v > 0]
        used = [d for d in K.dsems if d.count > 0]
        dfinals = [(d.sem, d.count) for d in used]
        K.dsems = []
        with nc.Block() as block:
            for eng in ENGS:
                def body(e, eng=eng):
                    known = {}
                    if pidx > 0:
                        e.wait_ge(K.bar, pidx)
                    for o in self.q[eng]:
                        for w in o.waits:
                            if known.get(id(w.sem), 0) < w.val:
                                e.wait_ge(w.sem, w.val)
                                known[id(w.sem)] = w.val
                        ins = o.fn(e)
                        if o.ev.needed:
                            ins.then_inc(o.ev.sem, 16 if o.ev.dma else 1)
                    if eng == "sync":
                        for (s_, v_) in finals + dfinals:
                            e.wait_ge(s_, v_)
                        for (s_, v_) in dfinals:
                            e.sem_clear(s_)
                        e.sem_inc(K.bar, 1)
                        e.wait_ge(K.bar, pidx + 1)
                getattr(block, eng)(body)
        K.phase_idx += 1
        for d in used:
            K.pool.append(d._sem)


def build(debug=False, cfg=None):
    cfg = cfg or {}
    nc = bass.Bass("TRN2", target_bir_lowering=False)
    stack = ExitStack()
    K = Kern(nc, stack)

    def din(name, shape, dt=F32):
        return nc.dram_tensor(name, list(shape), dt, kind="ExternalInput").ap()

    def dout(name, shape, dt=F32):
        return nc.dram_tensor(name, list(shape), dt, kind="ExternalOutput").ap()

    def dscr(name, shape, dt=BF16):
        return nc.dram_tensor(name, list(shape), dt, kind="ExternalOutput" if debug else "Internal").ap()

    x = din("x", [NTOK, D])
    w1f = din("w1f", [D, 12800])
    wg = din("wg", [D, 16])
    w1t = din("w1t", [D, 4352])
    bigf = din("bigf", [8, 2])
    wmn = din("wmn", [1, 2048])
    wa = din("wa", [2048, D])
    wb = din("wb", [2048, D])
    wo = din("wo", [D, D])
    wu = din("wu", [D, DFF])
    wd = din("wd", [DFF, D])
    lnp = din("lnp", [4, D])
    sinkr = din("sinkr", [1, 32])
    ck = din("ck", [128, 256])
    cv = din("cv", [128, 256])
    identf = din("identf", [128, 128])
    identb = din("identb", [128, 128], BF16)
    permb = din("permb", [128, 128], BF16)
    maskT = din("maskT", [128, 128])
    cosT = din("cosT", [128, NTOK])
    sinT = din("sinT", [128, NTOK])
    stC = din("stC", [8, 256, 128])
    stn = din("stn", [8, 128])
    stm = din("stm", [8, 1])
    y = dout("y", [NTOK, D])
    pk = dout("pk", [128, 256])
    pv = dout("pv", [128, 256])
    sk = dout("sk", [32, 256])
    sv = dout("sv", [32, 256])
    pC = dout("pC", [8, 256, 128])
    pn = dout("pn", [8, 128])
    pm = dout("pm", [8, 1])
    sC = dout("sC", [8, 256, 128])
    sn = dout("sn", [8, 128])
    sm = dout("sm", [8, 1])
    QT = dscr("QT", [8, 128, NTOK])
    KT = dscr("KT", [8, 128, NTOK])
    AQR = dscr("AQR", [16, 128, NTOK])
    AKR = dscr("AKR", [4, 128, NTOK])
    GT = dscr("GT", [64, 128, NTOK])
    IG = dscr("IG", [8, NTOK], F32)
    FP = dscr("FP", [8, NTOK], F32)
    MV = dscr("MV", [NTOK, 2048])
    SIGMO = dscr("SIGMO", [NTOK, 2048])
    AV = dscr("AV", [NTOK, 256])
    YT = dscr("YT", [32, 128, NTOK])
    HPRE = dscr("HPRE", [NTOK, D], F32)
    HH = dscr("HH", [NTOK, D], F32)
    HT = dscr("HT", [5, 128, 32, 512])
    UT = dscr("UT", [5, 128, 128, 512])
    FPRE = dscr("FPRE", [NTOK, D], F32)

    ps_t = stack.enter_context(nc.psum_tensor("ps", [128, 8, 512], F32))
    banks = [Buf(f"bank{i}") for i in range(8)]

    def sb(st, name, shape, dt):
        return st.enter_context(nc.sbuf_tensor(name, list(shape), dt))

    def wview(w, c0, ncol, k0=0, nk=32):
        return w.rearrange("(k p) n -> p k n", p=128)[:, k0:k0 + nk, c0:c0 + ncol]

    def X(P, eng, meth, args, reads=(), writes=(), pwrites=(), **kw):
        return P.op(eng, lambda e: getattr(e, meth)(*args, **kw), reads, writes, pwrites)

    def MM(P, bi, out_ap, pairs, reads):
        np_ = len(pairs)

        def fn(e):
            ins = None
            for i, (l, r) in enumerate(pairs):
                ins = e.matmul(out_ap, l, r, start=(i == 0), stop=(i == np_ - 1))
            return ins
        return P.op("tensor", fn, reads=reads, writes=[banks[bi]])

    def nbank(P):
        i = P.bank_i % 8
        P.bank_i += 1
        return i

    class Ring:
        def __init__(self, st, name, n, shape, dt):
            self.t = [sb(st, f"{name}{i}", shape, dt) for i in range(n)]
            self.b = [Buf(f"{name}{i}") for i in range(n)]
            self.d = [K.new_dsem(name) for i in range(n)]
            self.i = 0
            self.n = n

        def nxt(self):
            i = self.i % self.n
            self.i += 1
            return self.t[i], self.b[i], self.d[i]

    def phase1():
        P = Phase(K, "p1")
        st = ExitStack()
        xT = sb(st, "xT", [128, 32, 1056], BF16)
        XS = Ring(st, "xs", 2, [128, D], F32)
        WR = Ring(st, "wr", 2, [128, 32, 512], BF16)
        STB = Ring(st, "stb", 4, [128, 512], BF16)
        STF = Ring(st, "stf", 4, [128, 512], F32)
        wgt = sb(st, "wgt", [128, 32, 16], BF16)
        cst = sb(st, "cst", [128, 1056], F32)
        snt = sb(st, "snt", [128, 1056], F32)
        idf = sb(st, "idf", [128, 128], F32)
        bg = sb(st, "bg", [8, 2], F32)
        pmb = sb(st, "pmb", [128, 128], BF16)
        B_xT = Buf("xT")
        B_c = Buf("consts")
        B_tab = Buf("tab")
        D_c = K.new_dsem()

        P.dma("sync", idf[:], identf[:, :], K.new_dsem(), pwrites=[B_c])
        P.dma("sync", bg[:], bigf[:, :], K.new_dsem(), pwrites=[B_c])
        P.dma("sync", pmb[:], permb[:, :], K.new_dsem(), pwrites=[B_c])
        P.dma("gpsimd", wgt[:], wg.rearrange("(k p) n -> p k n", p=128), K.new_dsem(), pwrites=[B_c])

        for (h0, hn) in cfg.get('halves', HALVES):
            P.dma("sync", cst[:, 0:hn], cosT[:, h0:h0 + hn], K.new_dsem(), writes=[B_tab])
            P.dma("sync", snt[:, 0:hn], sinT[:, h0:h0 + hn], K.new_dsem(), pwrites=[B_tab])
            B_xT.new_gen()
            blocks = [(t, n) for (t, n) in TBLOCKS if h0 <= t < h0 + hn]
            for (t, n) in blocks:
                xs, bxs, dxs = XS.nxt()
                P.dma("sync", xs[0:n, :], x[t:t + n, :], dxs, writes=[bxs])
                for c0 in range(0, 32, 4):
                    bi = nbank(P)
                    trs = [(ps_t[:, bi, j * n:(j + 1) * n], xs[0:n, (c0 + j) * 128:(c0 + j + 1) * 128], idf[0:n, 0:n])
                           for j in range(4)]

                    def fn(e, trs=trs):
                        ins = None
                        for (o_, i_, d_) in trs:
                            ins = e.transpose(o_, i_, d_)
                        return ins
                    P.op("tensor", fn, reads=[bxs, B_c], writes=[banks[bi]])
                    src = ps_t[:, bi, 0:4 * n].rearrange("p (j n) -> p j n", j=4)
                    dst = xT[:, c0:c0 + 4, t - h0:t - h0 + n]
                    if (c0 // 4) % 2 == 0:
                        X(P, "vector", "tensor_copy", (dst, src), reads=[banks[bi]], pwrites=[B_xT])
                    else:
                        X(P, "scalar", "copy", (dst, src), reads=[banks[bi]], pwrites=[B_xT])
            tts = ttiles(h0, hn)

            def fm_group(wtile, bw, col0, m, t, n):
                bi = nbank(P)
                pairs = [(wtile[:, k, col0:col0 + m], xT[:, k, t - h0:t - h0 + n]) for k in range(32)]
                MM(P, bi, ps_t[0:m, bi, 0:n], pairs, reads=[bw, B_xT])
                return bi

            for gi, dst_d in cfg.get('gates', ((0, IG), (1, FP))):
                for (t, n) in tts:
                    bi = fm_group(wgt, B_c, gi * 8, 8, t, n)
                    s_, bs, ds = STF.nxt()
                    X(P, "vector", "tensor_scalar", (s_[0:8, 0:n], ps_t[0:8, bi, 0:n], bg[0:8, gi:gi + 1], None, ALU.add),
                      reads=[banks[bi], B_c], writes=[bs])
                    P.dma("sync", dst_d[:, t:t + n], s_[0:8, 0:n], ds, reads=[bs])

            for ti in cfg.get('fm', range(25)):
                wt, bw, dw = WR.nxt()
                P.dma("gpsimd", wt[:], wview(w1f, ti * 512, 512), dw, writes=[bw])
                for (t, n) in tts:
                    if ti < 4:
                        for j in range(4):
                            ch = ti * 4 + j
                            bi = fm_group(wt, bw, j * 128, 128, t, n)
                            s_, bs, ds = STB.nxt()
                            sc = (128.0 ** -0.5) if ch < 8 else 1.0
                            X(P, "vector", "tensor_scalar", (s_[:, 0:n], ps_t[:, bi, 0:n], sc, None, ALU.mult),
                              reads=[banks[bi]], writes=[bs])
                            dstT = QT if ch < 8 else KT
                            P.dma("sync", dstT[ch % 8, :, t:t + n], s_[:, 0:n], ds, reads=[bs])
                    elif ti < 9:
                        for jp in range(4):
                            pi = (ti - 4) * 4 + jp
                            b0 = fm_group(wt, bw, jp * 128, 128, t, n)
                            sp, bsp, _ = STB.nxt()
                            X(P, "scalar", "copy", (sp[:, 0:n], ps_t[:, b0, 0:n]), reads=[banks[b0]], writes=[bsp])
                            b1 = nbank(P)
                            MM(P, b1, ps_t[:, b1, 0:n], [(pmb[:, :], sp[:, 0:n])], reads=[B_c, bsp])
                            f0, bf0, _ = STF.nxt()
                            X(P, "vector", "tensor_tensor", (f0[:, 0:n], ps_t[:, b0, 0:n], cst[:, t - h0:t - h0 + n], ALU.mult),
                              reads=[banks[b0], B_tab, bsp], writes=[bf0])
                            f1, bf1, _ = STF.nxt()
                            X(P, "vector", "tensor_tensor", (f1[:, 0:n], ps_t[:, b1, 0:n], snt[:, t - h0:t - h0 + n], ALU.mult),
                              reads=[banks[b1], B_tab], writes=[bf1])
                            s_, bs, ds = STB.nxt()
                            X(P, "vector", "tensor_tensor", (s_[:, 0:n], f0[:, 0:n], f1[:, 0:n], ALU.add),
                              reads=[bf0, bf1], writes=[bs])
                            if pi < 16:
                                P.dma("sync", AQR[pi, :, t:t + n], s_[:, 0:n], ds, reads=[bs])
                            else:
                                kh = pi - 16
                                P.dma("sync", AKR[kh, :, t:t + n], s_[:, 0:n], ds, reads=[bs])
                                oc = None
                                if t == 1536:
                                    oc = (384, 128, pk)
                                elif t == 2048:
                                    oc = (0, 32, sk)
                                if oc is not None:
                                    c_, m_, dst_ = oc
                                    f2, bf2, _ = STF.nxt()
                                    X(P, "vector", "tensor_tensor", (f2[:, 0:m_], f0[:, c_:c_ + m_], f1[:, c_:c_ + m_], ALU.add),
                                      reads=[bf0, bf1], writes=[bf2])
                                    bt = nbank(P)
                                    X(P, "tensor", "transpose", (ps_t[0:128, bt, 0:128], f2[:, 0:128], idf[:, :]),
                                      reads=[bf2, B_c], writes=[banks[bt]])
                                    f3, bf3, df3 = STF.nxt()
                                    X(P, "vector", "tensor_copy", (f3[0:m_, 0:64], ps_t[0:m_, bt, 0:64]),
                                      reads=[banks[bt]], writes=[bf3])
                                    P.dma("sync", dst_[:, kh * 64:(kh + 1) * 64], f3[0:m_, 0:64], df3, reads=[bf3])
                    else:
                        for j in range(4):
                            ch = (ti - 9) * 4 + j
                            bi = fm_group(wt, bw, j * 128, 128, t, n)
                            s_, bs, ds = STB.nxt()
                            X(P, "scalar", "activation", (s_[:, 0:n], ps_t[:, bi, 0:n], AF.Sigmoid),
                              reads=[banks[bi]], writes=[bs])
                            P.dma("sync", GT[ch, :, t:t + n], s_[:, 0:n], ds, reads=[bs])

            for ti in cfg.get('tm', range(9)):
                ncol = 512 if ti < 8 else 256
                wt, bw, dw = WR.nxt()
                P.dma("gpsimd", wt[:, :, 0:ncol], wview(w1t, ti * 512, ncol), dw, writes=[bw])
                for (t, n) in blocks:
                    bi = nbank(P)
                    pairs = [(xT[:, k, t - h0:t - h0 + n], wt[:, k, 0:ncol]) for k in range(32)]
                    MM(P, bi, ps_t[0:n, bi, 0:ncol], pairs, reads=[bw, B_xT])
                    s_, bs, ds = STB.nxt()
                    if 4 <= ti < 8:
                        X(P, "scalar", "activation", (s_[0:n, 0:ncol], ps_t[0:n, bi, 0:ncol], AF.Sigmoid),
                          reads=[banks[bi]], writes=[bs])
                    else:
                        X(P, "vector", "tensor_copy", (s_[0:n, 0:ncol], ps_t[0:n, bi, 0:ncol]),
                          reads=[banks[bi]], writes=[bs])
                    if ti < 4:
                        P.dma("sync", MV[t:t + n, ti * 512:(ti + 1) * 512], s_[0:n, 0:512], ds, reads=[bs])
                    elif ti < 8:
                        P.dma("sync", SIGMO[t:t + n, (ti - 4) * 512:(ti - 3) * 512], s_[0:n, 0:512], ds, reads=[bs])
                    else:
                        P.dma("sync", AV[t:t + n, :], s_[0:n, 0:256], ds, reads=[bs])
                        if t == 1920 or t == 2048:
                            f2, bf2, df2 = STF.nxt()
                            X(P, "scalar", "copy", (f2[0:n, 0:256], ps_t[0:n, bi, 0:256]),
                              reads=[banks[bi], bs], writes=[bf2])
                            P.dma("sync", (pv if t == 1920 else sv)[:, :], f2[0:n, 0:256], df2, reads=[bf2])
        P.run()
        st.close()


    def phase2():
        P = Phase(K, "p2")
        st = ExitStack()
        GW = 2176
        tiles = {}

        def T(name, shape=None, dt=F32):
            t_ = sb(st, "m_" + name, shape or [8, GW], dt)
            tiles[name] = (t_, Buf(name))
            return tiles[name]
        ig, Big = T("ig"); fp, Bfp = T("fp"); l1, Bl1 = T("l1"); nb, Bnb = T("nb"); G, BG = T("G"); mu, Bmu = T("mu")
        wk, Bwk = T("wk"); rr, Brr = T("rr"); aa, Baa = T("aa"); fl, Bfl = T("fl"); ones, Bones = T("ones")
        ME, BME = T("ME", [8, 17]); MS, BMS = T("MS", [8, 17]); nME, BnME = T("nME", [8, 17]); dec, Bdec = T("dec", [8, 17])
        stm_t, Bstm = T("stm", [8, 1]); mo_t, Bmo = T("mo", [8, 2])
        rhsD, BrhsD = T("rhsD", [8, 8, 17])
        idf, Bidf = T("idf", [128, 128]); idb, Bidb = T("idb", [128, 128], BF16)
        msk, Bmsk = T("msk", [128, 128])
        cols, Bcols = T("cols", [128, 17, 32]); decb, Bdecb = T("decb", [128, 8, 17])
        wmnb, Bwmnb = T("wmnb", [128, 2048])
        CTf = sb(st, "CTf", [128, 8, 257], F32)
        CTb = sb(st, "CTb", [128, 8, 257], BF16)
        BCf = [Buf() for _ in range(8)]
        BCb = [Buf() for _ in range(8)]
        sct, Bsct = T("sct", [128, 8, 2, 128])
        npad, Bnpad = T("npad", [128, 128])
        stn_t, Bstn = T("stn", [8, 128])
        QR = Ring(st, "qt", 2, [128, 8, 128], BF16)
        KR = Ring(st, "kt", 2, [128, 8, 128], BF16)
        VR = Ring(st, "vv", 2, [128, 8, 257], BF16)
        SG = Ring(st, "sg", 2, [128, 2048], BF16)
        GWR = Ring(st, "gw", 1, [128, 2048], BF16)
        STM = Ring(st, "stm", 3, [128, 128], BF16)
        KWR = Ring(st, "kw", 3, [128, 128], BF16)
        TMP = Ring(st, "tmp", 2, [128, 257], F32)
        HNR = Ring(st, "hn", 1, [128, 8, 257], F32)
        SM = Ring(st, "sm", 2, [128, 64], F32)
        SQ = Ring(st, "sq", 2, [128, 256], F32)
        YA = Ring(st, "ya", 2, [128, 2048], BF16)
        YAT = Ring(st, "yat", 2, [128, 16, 128], BF16)
        SO = Ring(st, "so", 2, [128, 2, 128], F32)
        D_c = K.new_dsem()

        def V(meth, args, reads=(), writes=(), pwrites=(), **kw):
            return X(P, "vector", meth, args, reads, writes, pwrites, **kw)

        def A(meth, args, reads=(), writes=(), pwrites=(), **kw):
            return X(P, "scalar", meth, args, reads, writes, pwrites, **kw)

        def G_(meth, args, reads=(), writes=(), pwrites=(), **kw):
            return X(P, "gpsimd", meth, args, reads, writes, pwrites, **kw)

        P.dma("sync", ig[:, 0:NTOK], IG[:, :], K.new_dsem(), writes=[Big])
        P.dma("sync", fp[:, 0:NTOK], FP[:, :], K.new_dsem(), writes=[Bfp])
        P.dma("sync", stm_t[:], stm[:, :], K.new_dsem(), writes=[Bstm])
        P.dma("sync", idf[:], identf[:, :], K.new_dsem(), writes=[Bidf])
        P.dma("sync", idb[:], identb[:, :], K.new_dsem(), writes=[Bidb])
        P.dma("sync", msk[:], maskT[:, :], K.new_dsem(), writes=[Bmsk])
        P.dma("sync", wmnb[:], wmn[0, :].partition_broadcast(128), K.new_dsem(), writes=[Bwmnb])
        G_("memset", (ones[:], 1.0), writes=[Bones])
        for (t_, b_) in ((wk, Bwk), (rr, Brr), (aa, Baa), (fl, Bfl)):
            G_("memset", (t_[:], 0.0), writes=[b_])
        G_("memset", (npad[:], 0.0), writes=[Bnpad])
        A("activation", (l1[:, 0:NTOK], fp[:, 0:NTOK], AF.Exp), reads=[Bfp], writes=[Bl1], scale=-1.0)
        A("activation", (l1[:, 0:NTOK], l1[:, 0:NTOK], AF.Ln), reads=[Bl1], writes=[Bl1], bias=1.0)
        Bnb.new_gen()
        V("tensor_tensor_scan", (nb[:, 0:S], ones[:, 0:S], l1[:, 0:S], 0.0, ALU.mult, ALU.add), reads=[Bones, Bl1], pwrites=[Bnb])
        V("tensor_tensor_scan", (nb[:, S:NTOK], ones[:, S:NTOK], l1[:, S:NTOK], 0.0, ALU.mult, ALU.add), reads=[Bones, Bl1], pwrites=[Bnb])
        V("tensor_tensor", (G[:, 0:NTOK], ig[:, 0:NTOK], nb[:, 0:NTOK], ALU.add), reads=[Big, Bnb], writes=[BG])
        Bmu.new_gen()
        V("tensor_tensor_scan", (mu[:, 0:S], ones[:, 0:S], G[:, 0:S], 0.0, ALU.mult, ALU.max), reads=[Bones, BG], pwrites=[Bmu])
        V("tensor_tensor_scan", (mu[:, S:NTOK], ones[:, S:NTOK], G[:, S:NTOK], stm_t[:, 0:1], ALU.mult, ALU.max),
          reads=[Bones, BG, Bstm], pwrites=[Bmu])
        BME.new_gen()
        V("tensor_copy", (ME[:, 0:16], mu[:, 0:S].rearrange("p (c t) -> p c t", t=128)[:, :, 127]), reads=[Bmu], pwrites=[BME])
        V("tensor_copy", (ME[:, 16:17], mu[:, NTOK - 1:NTOK]), reads=[Bmu], pwrites=[BME])
        BMS.new_gen()
        G_("memset", (MS[:, 0:1], 0.0), pwrites=[BMS])
        V("tensor_copy", (MS[:, 1:16], ME[:, 0:15]), reads=[BME], pwrites=[BMS])
        V("tensor_copy", (MS[:, 16:17], stm_t[:, 0:1]), reads=[Bstm], pwrites=[BMS])
        V("tensor_scalar", (nME[:], ME[:], -1.0, None, ALU.mult), reads=[BME], writes=[BnME])
        V("tensor_tensor", (dec[:], MS[:], ME[:], ALU.subtract), reads=[BMS, BME], writes=[Bdec])
        A("activation", (dec[:], dec[:], AF.Exp), reads=[Bdec], writes=[Bdec])
        V("tensor_tensor", (fl[:, 0:NTOK], nb[:, 0:NTOK], mu[:, 0:NTOK], ALU.subtract), reads=[Bnb, Bmu], writes=[Bfl])
        A("activation", (fl[:, 0:NTOK], fl[:, 0:NTOK], AF.Exp), reads=[Bfl], writes=[Bfl])
        Bwk.new_gen(); Brr.new_gen(); Baa.new_gen()
        for c, (t, n) in enumerate(TBLOCKS):
            A("activation", (wk[:, t:t + n], G[:, t:t + n], AF.Exp), reads=[BG, BnME], pwrites=[Bwk], bias=nME[:, c:c + 1])
            A("activation", (rr[:, t:t + n], mu[:, t:t + n], AF.Exp), reads=[Bmu, BME], pwrites=[Brr], bias=ME[:, c:c + 1], scale=-1.0)
            A("activation", (aa[:, t:t + n], mu[:, t:t + n], AF.Exp), reads=[Bmu, BMS], pwrites=[Baa], bias=MS[:, c:c + 1], scale=-1.0)
        Bmo.new_gen()
        V("tensor_tensor", (mo_t[:, 0:1], mu[:, S - 1:S], nb[:, S - 1:S], ALU.subtract), reads=[Bmu, Bnb], pwrites=[Bmo])
        V("tensor_tensor", (mo_t[:, 1:2], mu[:, NTOK - 1:NTOK], nb[:, NTOK - 1:NTOK], ALU.subtract), reads=[Bmu, Bnb], pwrites=[Bmo])
        P.dma("sync", pm[:, :], mo_t[:, 0:1], K.new_dsem(), reads=[Bmo])
        P.dma("sync", sm[:, :], mo_t[:, 1:2], K.new_dsem(), reads=[Bmo])
        Bcols.new_gen()
        for c, (t, n) in enumerate(TBLOCKS):
            bi = nbank(P)
            qs = [(wk, Bwk), (rr, Brr), (aa, Baa), (fl, Bfl)]
            trs = [(ps_t[0:128, bi, q * 8:(q + 1) * 8], qt_[0:8, t:t + 128], idf[0:8, 0:8]) for q, (qt_, _) in enumerate(qs)]

            def fn(e, trs=trs):
                ins = None
                for (o_, i_, d_) in trs:
                    ins = e.transpose(o_, i_, d_)
                return ins
            P.op("tensor", fn, reads=[Bwk, Brr, Baa, Bfl, Bidf], writes=[banks[bi]])
            V("tensor_copy", (cols[:, c, :], ps_t[:, bi, 0:32]), reads=[banks[bi]], pwrites=[Bcols])
        BrhsD.new_gen()
        for h in range(8):
            V("tensor_scalar", (rhsD[:, h, :], dec[:, :], idf[0:8, h:h + 1], None, ALU.mult), reads=[Bdec, Bidf], pwrites=[BrhsD])
        bi = nbank(P)
        MM(P, bi, ps_t[:, bi, 0:136], [(ones[0:8, 0:128], rhsD[:].rearrange("p h c -> p (h c)"))], reads=[Bones, BrhsD])
        V("tensor_copy", (decb[:].rearrange("p h c -> p (h c)"), ps_t[:, bi, 0:136]), reads=[banks[bi]], writes=[Bdecb])
        for i in range(2):
            G_("memset", (VR.t[i][:, :, 256:257], 1.0), writes=[VR.b[i]])
        for h in range(8):
            G_("memset", (CTf[:, h, :], 0.0), writes=[BCf[h]])
            G_("memset", (CTb[:, h, :], 0.0), writes=[BCb[h]])

        def emit_state(dC, dn):
            for h in range(8):
                bi = nbank(P)
                trs = [(ps_t[:, bi, vb * 128:(vb + 1) * 128], CTf[:, h, vb * 128:(vb + 1) * 128], idf[:, :]) for vb in range(2)]

                def fn(e, trs=trs):
                    ins = None
                    for (o_, i_, d_) in trs:
                        ins = e.transpose(o_, i_, d_)
                    return ins
                P.op("tensor", fn, reads=[BCf[h], Bidf], writes=[banks[bi]])
                so, bso, dso = SO.nxt()
                V("tensor_copy", (so[:].rearrange("p a b -> p (a b)"), ps_t[:, bi, 0:256]), reads=[banks[bi]], writes=[bso])
                P.dma("sync", dC[h].rearrange("(vb p) d -> p vb d", p=128), so[:], dso, reads=[bso])
            V("tensor_copy", (npad[:, 0:8], CTf[:, :, 256]), reads=BCf, writes=[Bnpad])
            bi = nbank(P)
            X(P, "tensor", "transpose", (ps_t[:, bi, 0:128], npad[:, :], idf[:, :]), reads=[Bnpad, Bidf], writes=[banks[bi]])
            so, bso, dso = SO.nxt()
            V("tensor_copy", (so[0:8, 0, :], ps_t[0:8, bi, 0:128]), reads=[banks[bi]], writes=[bso])
            P.dma("sync", dn[:, :], so[0:8, 0, :], dso, reads=[bso])

        def load_sample_state():
            P.dma("sync", sct[:], stC.rearrange("h (vb p) d -> p h vb d", p=128), K.new_dsem(), writes=[Bsct])
            P.dma("sync", stn_t[:], stn[:, :], K.new_dsem(), writes=[Bstn])
            for h in range(8):
                bi = nbank(P)
                trs = [(ps_t[:, bi, vb * 128:(vb + 1) * 128], sct[:, h, vb, :], idf[:, :]) for vb in range(2)]

                def fn(e, trs=trs):
                    ins = None
                    for (o_, i_, d_) in trs:
                        ins = e.transpose(o_, i_, d_)
                    return ins
                P.op("tensor", fn, reads=[Bsct, Bidf], writes=[banks[bi]])
                V("tensor_copy", (CTf[:, h, 0:256], ps_t[:, bi, 0:256]), reads=[banks[bi]], writes=[BCf[h]])
            bi = nbank(P)
            X(P, "tensor", "transpose", (ps_t[:, bi, 0:8], stn_t[0:8, :], idf[0:8, 0:8]), reads=[Bstn, Bidf], writes=[banks[bi]])
            V("tensor_copy", (CTf[:, :, 256], ps_t[:, bi, 0:8]), reads=[banks[bi]], pwrites=BCf)
            for h in range(8):
                A("copy", (CTb[:, h, :], CTf[:, h, :]), reads=[BCf[h]], writes=[BCb[h]])

        for c, (t, n) in enumerate(TBLOCKS):
            if c == 16:
                emit_state(pC, pn)
                load_sample_state()
            qt, bq, dq = QR.nxt()
            kt, bk, dk = KR.nxt()
            vv, bv, dv = VR.nxt()
            sg, bsg, dsg = SG.nxt()
            P.dma("sync", qt[:, :, 0:n], QT[:, :, t:t + n].rearrange("h p t -> p h t"), dq, writes=[bq])
            P.dma("sync", kt[:, :, 0:n], KT[:, :, t:t + n].rearrange("h p t -> p h t"), dk, writes=[bk])
            P.dma("sync", vv[0:n, :, 0:256], MV[t:t + n, :].rearrange("t (h v) -> t h v", h=8), dv, writes=[bv])
            P.dma("sync", sg[0:n, :], SIGMO[t:t + n, :], dsg, writes=[bsg])
            gw, bgw, _ = GWR.nxt()
            G_("tensor_tensor", (gw[0:n, :], sg[0:n, :], wmnb[0:n, :], ALU.mult), reads=[bsg, Bwmnb], writes=[bgw])
            hn, bhn, _ = HNR.nxt()
            sm_, bsm, _ = SM.nxt()
            bhn.new_gen()
            bsm.new_gen()
            for h in range(8):
                b_s = nbank(P)
                MM(P, b_s, ps_t[0:n, b_s, 0:n], [(kt[:, h, 0:n], qt[:, h, 0:n])], reads=[bk, bq])
                stm_, bstm_, _ = STM.nxt()
                V("scalar_tensor_tensor", (stm_[0:n, 0:n], ps_t[0:n, b_s, 0:n], cols[0:n, c, h:h + 1], msk[0:n, 0:n], ALU.mult, ALU.mult),
                  reads=[banks[b_s], Bcols, Bmsk], writes=[bstm_])
                b_i = nbank(P)
                MM(P, b_i, ps_t[0:n, b_i, 0:257], [(qt[:, h, 0:n], CTb[:, h, :])], reads=[bq, BCb[h]])
                b_a = nbank(P)
                MM(P, b_a, ps_t[0:n, b_a, 0:257], [(stm_[0:n, 0:n], vv[0:n, h, :])], reads=[bstm_, bv])
                tmp, btmp, _ = TMP.nxt()
                V("tensor_scalar", (tmp[0:n, :], ps_t[0:n, b_i, 0:257], cols[0:n, c, 16 + h:17 + h], None, ALU.mult),
                  reads=[banks[b_i], Bcols], writes=[btmp])
                V("scalar_tensor_tensor", (hn[0:n, h, :], ps_t[0:n, b_a, 0:257], cols[0:n, c, 8 + h:9 + h], tmp[0:n, :], ALU.mult, ALU.add),
                  reads=[banks[b_a], Bcols, btmp], pwrites=[bhn])
                sq, bsq, _ = SQ.nxt()
                A("activation", (sq[0:n, :], hn[0:n, h, 0:256], AF.Square), reads=[bhn], writes=[bsq], pwrites=[bsm],
                  accum_out=sm_[0:n, h:h + 1])
                kw, bkw, _ = KWR.nxt()
                b_t = nbank(P)
                pkb = ps_t[:, b_t, :].bitcast(BF16)
                X(P, "tensor", "transpose", (pkb[0:n, 0:128], kt[:, h, 0:n], idb[:, :]), reads=[bk, Bidb], writes=[banks[b_t]])
                A("activation", (kw[0:n, :], pkb[0:n, 0:128], AF.Copy), reads=[banks[b_t], Bcols], writes=[bkw],
                  scale=cols[0:n, c, h:h + 1])
                b_d = nbank(P)
                MM(P, b_d, ps_t[:, b_d, 0:257], [(kw[0:n, :], vv[0:n, h, :])], reads=[bkw, bv])
                V("scalar_tensor_tensor", (CTf[:, h, :], CTf[:, h, :], decb[:, h, c:c + 1], ps_t[:, b_d, 0:257], ALU.mult, ALU.add),
                  reads=[banks[b_d], Bdecb, BCf[h]], writes=[BCf[h]])
                A("copy", (CTb[:, h, :], CTf[:, h, :]), reads=[BCf[h]], writes=[BCb[h]])
            dd = sm_[0:n, 8:16]
            tt_ = sm_[0:n, 16:24]
            ff = sm_[0:n, 24:32]
            ss = sm_[0:n, 0:8]
            V("tensor_scalar", (tt_, hn[0:n, :, 256], -1.0, None, ALU.mult), reads=[bhn], pwrites=[bsm])
            V("tensor_tensor", (dd, tt_, hn[0:n, :, 256], ALU.max), reads=[bhn, bsm], writes=[bsm])
            V("tensor_tensor", (dd, dd, cols[0:n, c, 24:32], ALU.max), reads=[bsm, Bcols], writes=[bsm])
            V("reciprocal", (dd, dd), reads=[bsm], writes=[bsm])
            V("tensor_tensor", (tt_, ss, dd, ALU.mult), reads=[bsm], writes=[bsm])
            V("tensor_tensor", (tt_, tt_, dd, ALU.mult), reads=[bsm], writes=[bsm])
            V("tensor_scalar", (tt_, tt_, 1.0 / 256.0, RMS_EPS, ALU.mult, ALU.add), reads=[bsm], writes=[bsm])
            A("activation", (tt_, tt_, AF.Sqrt), reads=[bsm], writes=[bsm])
            V("reciprocal", (tt_, tt_), reads=[bsm], writes=[bsm])
            V("tensor_tensor", (ff, tt_, dd, ALU.mult), reads=[bsm], writes=[bsm])
            ya, bya, _ = YA.nxt()
            bya.new_gen()
            for h in range(8):
                V("scalar_tensor_tensor", (ya[0:n, h * 256:(h + 1) * 256], hn[0:n, h, 0:256], sm_[0:n, 24 + h:25 + h],
                                           gw[0:n, h * 256:(h + 1) * 256], ALU.mult, ALU.mult),
                  reads=[bhn, bsm, bgw], pwrites=[bya])
            yat, byat, dyat = YAT.nxt()
            byat.new_gen()
            for j0 in (0, 8):
                bi = nbank(P)
                psb = ps_t[:, bi, :].bitcast(BF16)
                trs = [(psb[:, j * 128:j * 128 + n], ya[0:n, (j0 + j) * 128:(j0 + j + 1) * 128], idb[0:n, 0:n]) for j in range(8)]

                def fn(e, trs=trs):
                    ins = None
                    for (o_, i_, d_) in trs:
                        ins = e.transpose(o_, i_, d_)
                    return ins
                P.op("tensor", fn, reads=[bya, Bidb], writes=[banks[bi]])
                A("copy", (yat[:, j0:j0 + 8, 0:n], psb[:, 0:1024].rearrange("p (j t) -> p j t", t=128)[:, :, 0:n]),
                  reads=[banks[bi]], pwrites=[byat])
            P.dma("sync", YT[0:16, :, t:t + n].rearrange("c p t -> p c t"), yat[:, :, 0:n], dyat, reads=[byat])
        emit_state(sC, sn)
        P.run()
        st.close()


    def phase3():
        P = Phase(K, "p3")
        st = ExitStack()
        AQ = Ring(st, "aq", 2, [128, 16, 128], BF16)
        AK = Ring(st, "ak", 3, [128, 4, 128], BF16)
        VA = Ring(st, "va", 3, [128, 4, 65], BF16)
        PT = Ring(st, "pt", 8, [128, 4, 128], BF16)
        YB = Ring(st, "yb", 2, [128, 2048], BF16)
        YBT = Ring(st, "ybt", 2, [128, 16, 128], BF16)
        DN = Ring(st, "dn", 4, [128, 8], F32)
        esb = sb(st, "esb", [128, 32], F32); Besb = Buf()
        idf = sb(st, "idf3", [128, 128], F32); Bidf = Buf()
        idb = sb(st, "idb3", [128, 128], BF16); Bidb = Buf()
        ckd = sb(st, "ckd", [128, 4, 2, 64], F32); Bckd = Buf()
        akc = sb(st, "akc", [128, 4, 128], BF16); Bakc = Buf()
        vac = sb(st, "vac", [128, 4, 65], BF16); Bvac = Buf()
        D_c = K.new_dsem()

        def V(meth, args, reads=(), writes=(), pwrites=(), **kw):
            return X(P, "vector", meth, args, reads, writes, pwrites, **kw)

        def A(meth, args, reads=(), writes=(), pwrites=(), **kw):
            return X(P, "scalar", meth, args, reads, writes, pwrites, **kw)

        def G_(meth, args, reads=(), writes=(), pwrites=(), **kw):
            return X(P, "gpsimd", meth, args, reads, writes, pwrites, **kw)

        P.dma("sync", esb[:], sinkr[0, :].partition_broadcast(128), K.new_dsem(), writes=[Besb])
        A("activation", (esb[:], esb[:], AF.Exp), reads=[Besb], writes=[Besb])
        P.dma("sync", idf[:], identf[:, :], K.new_dsem(), writes=[Bidf])
        P.dma("sync", idb[:], identb[:, :], K.new_dsem(), writes=[Bidb])
        for i in range(3):
            G_("memset", (VA.t[i][:, :, 64:65], 1.0), writes=[VA.b[i]])
        G_("memset", (vac[:, :, 64:65], 1.0), writes=[Bvac])
        Bckd.new_gen()
        ckv = ck.rearrange("j (k d) -> j k d", k=4)
        P.dma("sync", ckd[:, :, 0, :], ckv, K.new_dsem(), pwrites=[Bckd])
        P.dma("sync", ckd[:, :, 1, :], ckv, K.new_dsem(), pwrites=[Bckd])
        P.dma("gpsimd", vac[:, :, 0:64], cv.rearrange("j (k d) -> j k d", k=4), K.new_dsem(), reads=[], writes=[Bvac])
        bi = nbank(P)
        trs = [(ps_t[:, bi, kh * 128:(kh + 1) * 128], ckd[:, kh, :, :].rearrange("p a d -> p (a d)"), idf[:, :]) for kh in range(4)]

        def fnc(e, trs=trs):
            ins = None
            for (o_, i_, d_) in trs:
                ins = e.transpose(o_, i_, d_)
            return ins
        P.op("tensor", fnc, reads=[Bckd, Bidf], writes=[banks[bi]])
        V("tensor_copy", (akc[:].rearrange("p k j -> p (k j)"), ps_t[:, bi, 0:512]), reads=[banks[bi]], writes=[Bakc])

        prev = None
        for c, (t, n) in enumerate(TBLOCKS):
            aq, baq, daq = AQ.nxt()
            P.dma("sync", aq[:, :, 0:n], AQR[:, :, t:t + n].rearrange("c p t -> p c t"), daq, writes=[baq])
            ak, bak, dak = AK.nxt()
            P.dma("sync", ak[:, :, 0:n], AKR[:, :, t:t + n].rearrange("c p t -> p c t"), dak, writes=[bak])
            va, bva, dva = VA.nxt()
            P.dma("sync", va[0:n, :, 0:64], AV[t:t + n, :].rearrange("t (k d) -> t k d", k=4), dva, writes=[bva])
            if c == 16:
                ktiles = [(akc, Bakc, vac, Bvac, 128, None), (ak, bak, va, bva, 128, "new")]
            elif c == 0:
                ktiles = [(ak, bak, va, bva, 128, "cur")]
            else:
                ktiles = [prev + (128, "prev"), (ak, bak, va, bva, 128, "cur")]
            prev = (ak, bak, va, bva)
            yb, byb, _ = YB.nxt()
            byb.new_gen()
            for kvh in range(4):
                pts = {}
                for ki, (akt, bakt, vat, bvat, nj, role) in enumerate(ktiles):
                    for par in range(2):
                        bs_ = nbank(P)
                        o3 = ps_t[0:nj, bs_, 0:4 * n].rearrange("p (a q) -> p a q", a=4)
                        MM(P, bs_, o3, [(akt[par * 64:(par + 1) * 64, kvh, 0:nj],
                                         aq[par * 64:(par + 1) * 64, kvh * 4:(kvh + 1) * 4, 0:n])], reads=[bakt, baq])
                        pt, bpt, _ = PT.nxt()
                        A("activation", (pt[0:nj, :, 0:n], o3, AF.Exp), reads=[banks[bs_]], writes=[bpt], scale=0.125)
                        if role == "prev":
                            G_("memset", (pt[0:64, :, 64:128], 0.0), reads=[bpt], pwrites=[bpt])
                        elif role == "new":
                            G_("memset", (pt[32:64, :, 0:n], 0.0), reads=[bpt], pwrites=[bpt])
                            G_("memset", (pt[64:128, :, 0:n], 0.0), reads=[bpt], pwrites=[bpt])
                        elif role == "cur":
                            G_("memset", (pt[64:128, :, 0:64], 0.0), reads=[bpt], pwrites=[bpt])
                        pts[(ki, par)] = (pt, bpt)
                for par in range(2):
                    bo = nbank(P)
                    mms = []
                    rds = []
                    for pair in range(4):
                        nk = len(ktiles)
                        for ki, (akt, bakt, vat, bvat, nj, role) in enumerate(ktiles):
                            pt, bpt = pts[(ki, par)]
                            mms.append((ps_t[0:n, bo, pair * 65:(pair + 1) * 65], pt[0:nj, pair, 0:n], vat[0:nj, kvh, :],
                                        ki == 0, ki == nk - 1))
                            rds += [bpt, bvat]

                    def fnv(e, mms=mms):
                        ins = None
                        for (o_, l_, r_, s0, s1) in mms:
                            ins = e.matmul(o_, l_, r_, start=s0, stop=s1)
                        return ins
                    P.op("tensor", fnv, reads=rds, writes=[banks[bo]])
                    dn, bdn, _ = DN.nxt()
                    hb = kvh * 8 + par * 4
                    V("tensor_tensor", (dn[0:n, 0:4], ps_t[0:n, bo, 0:260].rearrange("p (a c) -> p a c", c=65)[:, :, 64],
                                        esb[0:n, hb:hb + 4], ALU.add), reads=[banks[bo], Besb], writes=[bdn])
                    V("reciprocal", (dn[0:n, 0:4], dn[0:n, 0:4]), reads=[bdn], writes=[bdn])
                    for pair in range(4):
                        h = kvh * 8 + pair * 2 + par
                        V("tensor_scalar", (yb[0:n, h * 64:(h + 1) * 64], ps_t[0:n, bo, pair * 65:pair * 65 + 64],
                                            dn[0:n, pair:pair + 1], None, ALU.mult), reads=[banks[bo], bdn], pwrites=[byb])
            ybt, bybt, dybt = YBT.nxt()
            bybt.new_gen()
            for j0 in (0, 8):
                bi = nbank(P)
                psb = ps_t[:, bi, :].bitcast(BF16)
                trs = [(psb[:, j * 128:j * 128 + n], yb[0:n, (j0 + j) * 128:(j0 + j + 1) * 128], idb[0:n, 0:n]) for j in range(8)]

                def fn(e, trs=trs):
                    ins = None
                    for (o_, i_, d_) in trs:
                        ins = e.transpose(o_, i_, d_)
                    return ins
                P.op("tensor", fn, reads=[byb, Bidb], writes=[banks[bi]])
                A("copy", (ybt[:, j0:j0 + 8, 0:n], psb[:, 0:1024].rearrange("p (j t) -> p j t", t=128)[:, :, 0:n]),
                  reads=[banks[bi]], pwrites=[bybt])
            P.dma("sync", YT[16:32, :, t:t + n].rearrange("c p t -> p c t"), ybt[:, :, 0:n], dybt, reads=[bybt])
        P.run()
        st.close()


    P4_GROUPS = ((0, 768), (768, 768), (1536, 544))

    def phase4():
        P = Phase(K, "p4")
        st = ExitStack()
        YTs = sb(st, "yts", [128, 32, 768], BF16); B_yts = Buf()
        MT = sb(st, "mt", [128, 32, 768], BF16); B_mt = Buf()
        WR = Ring(st, "wr4", 2, [128, 32, 512], BF16)
        GTR = Ring(st, "gtr", 2, [128, 2, 4, 512], BF16)
        XSR = Ring(st, "xs4", 3, [128, 512], F32)
        STF = Ring(st, "stf4", 4, [128, 512], F32)
        B_yp = [Buf() for _ in range(2)]
        for (h0, hn) in cfg.get('p4g', P4_GROUPS):
            tts = ttiles(h0, hn)
            for i_, (t, n) in enumerate(tts):
                P.dma("sync", YTs[:, :, t - h0:t - h0 + n], YT[:, :, t:t + n].rearrange("c p t -> p c t"), K.new_dsem(),
                      writes=[B_yp[i_]])
            B_mt.new_gen()
            for et in range(8):
                wt, bw, dw = WR.nxt()
                P.dma("gpsimd", wt[:, 0:16, :], wview(wa, et * 512, 512, 0, 16), dw, writes=[bw])
                P.dma("gpsimd", wt[:, 16:32, :], wview(wb, et * 512, 512, 0, 16), dw, pwrites=[bw])
                for i_, (t, n) in enumerate(tts):
                    gt, bg_, dg = GTR.nxt()
                    P.dma("sync", gt[:, 0, :, 0:n], GT[et * 4:(et + 1) * 4, :, t:t + n].rearrange("c p t -> p c t"), dg, writes=[bg_])
                    P.dma("sync", gt[:, 1, :, 0:n], GT[32 + et * 4:32 + (et + 1) * 4, :, t:t + n].rearrange("c p t -> p c t"), dg, pwrites=[bg_])
                    for j in range(4):
                        ch = et * 4 + j
                        ba = nbank(P)
                        MM(P, ba, ps_t[:, ba, 0:n], [(wt[:, k, j * 128:(j + 1) * 128], YTs[:, k, t - h0:t - h0 + n]) for k in range(16)],
                           reads=[bw, B_yp[i_]])
                        bb = nbank(P)
                        MM(P, bb, ps_t[:, bb, 0:n], [(wt[:, k, j * 128:(j + 1) * 128], YTs[:, k, t - h0:t - h0 + n]) for k in range(16, 32)],
                           reads=[bw, B_yp[i_]])
                        f0, bf0, _ = STF.nxt()
                        X(P, "vector", "tensor_tensor", (f0[:, 0:n], ps_t[:, ba, 0:n], gt[:, 0, j, 0:n], ALU.mult),
                          reads=[banks[ba], bg_], writes=[bf0])
                        f1, bf1, _ = STF.nxt()
                        X(P, "vector", "tensor_tensor", (f1[:, 0:n], ps_t[:, bb, 0:n], gt[:, 1, j, 0:n], ALU.mult),
                          reads=[banks[bb], bg_], writes=[bf1])
                        X(P, "vector", "tensor_tensor", (MT[:, ch, t - h0:t - h0 + n], f0[:, 0:n], f1[:, 0:n], ALU.add),
                          reads=[bf0, bf1], pwrites=[B_mt])
            blocks = [(t, n) for (t, n) in TBLOCKS if h0 <= t < h0 + hn]
            for ct in range(8):
                wt, bw, dw = WR.nxt()
                P.dma("gpsimd", wt[:], wview(wo, ct * 512, 512), dw, writes=[bw])
                for (t, n) in blocks:
                    xs, bxs, dxs = XSR.nxt()
                    P.dma("sync", xs[0:n, :], x[t:t + n, ct * 512:(ct + 1) * 512], dxs, writes=[bxs])
                    bi = nbank(P)
                    MM(P, bi, ps_t[0:n, bi, 0:512], [(MT[:, k, t - h0:t - h0 + n], wt[:, k, :]) for k in range(32)], reads=[bw, B_mt])
                    f, bf, df = STF.nxt()
                    X(P, "vector", "scalar_tensor_tensor", (f[0:n, :], xs[0:n, :], ALPHA, ps_t[0:n, bi, 0:512], ALU.mult, ALU.add),
                      reads=[bxs, banks[bi]], writes=[bf])
                    P.dma("scalar", HPRE[t:t + n, ct * 512:(ct + 1) * 512], f[0:n, :], df, reads=[bf])
        P.run()
        st.close()

    def ln_phase(name, src_d, gi, dst_d, do_T):
        P = Phase(K, name)
        st = ExitStack()
        lng = sb(st, name + "g", [128, D], F32); Bg = Buf()
        lnb = sb(st, name + "b", [128, D], F32); Bb = Buf()
        idf = sb(st, name + "id", [128, 128], F32); Bidf = Buf()
        HP = Ring(st, name + "hp", 2, [128, D], F32)
        HN = Ring(st, name + "hn", 2, [128, D], F32)
        SS = Ring(st, name + "ss", 2, [128, 8], F32)
        if do_T:
            HTS = Ring(st, name + "ht", 2, [128, 32, 128], BF16)
        P.dma("sync", lng[:], lnp[gi, :].partition_broadcast(128), K.new_dsem(), writes=[Bg])
        P.dma("sync", lnb[:], lnp[gi + 1, :].partition_broadcast(128), K.new_dsem(), writes=[Bb])
        P.dma("sync", idf[:], identf[:, :], K.new_dsem(), writes=[Bidf])
        for (t, n) in cfg.get('lnblocks', TBLOCKS):
            hp, bhp, dhp = HP.nxt()
            P.dma("sync", hp[0:n, :], src_d[t:t + n, :], dhp, writes=[bhp])
            hn, bhn, dhn = HN.nxt()
            ss, bss, _ = SS.nxt()
            bss.new_gen()
            X(P, "scalar", "activation", (hn[0:n, :], hp[0:n, :], AF.Identity), reads=[bhp], writes=[bhn], pwrites=[bss],
              accum_out=ss[0:n, 0:1])
            X(P, "scalar", "activation", (hn[0:n, :], hp[0:n, :], AF.Square), reads=[bhp], writes=[bhn], pwrites=[bss],
              accum_out=ss[0:n, 1:2])
            mean = ss[0:n, 2:3]
            msq = ss[0:n, 3:4]
            var = ss[0:n, 4:5]
            X(P, "vector", "tensor_scalar", (mean, ss[0:n, 0:1], 1.0 / D, None, ALU.mult), reads=[bss], writes=[bss])
            X(P, "vector", "tensor_tensor", (msq, mean, mean, ALU.mult), reads=[bss], writes=[bss])
            X(P, "vector", "scalar_tensor_tensor", (var, ss[0:n, 1:2], 1.0 / D, msq, ALU.mult, ALU.subtract), reads=[bss], writes=[bss])
            X(P, "vector", "tensor_scalar", (var, var, LN_EPS, None, ALU.add), reads=[bss], writes=[bss])
            X(P, "scalar", "activation", (var, var, AF.Sqrt), reads=[bss], writes=[bss])
            X(P, "vector", "reciprocal", (var, var), reads=[bss], writes=[bss])
            X(P, "vector", "scalar_tensor_tensor", (hn[0:n, :], hp[0:n, :], mean, lng[0:n, :], ALU.subtract, ALU.mult),
              reads=[bhp, bss, bhn, Bg], writes=[bhn])
            X(P, "vector", "scalar_tensor_tensor", (hn[0:n, :], hn[0:n, :], var, lnb[0:n, :], ALU.mult, ALU.add),
              reads=[bhn, bss, Bb], writes=[bhn])
            P.dma("scalar", dst_d[t:t + n, :], hn[0:n, :], dhn, reads=[bhn])
            if do_T:
                hts, bhts, dhts = HTS.nxt()
                bhts.new_gen()
                for c0 in range(0, 32, 4):
                    bi = nbank(P)
                    trs = [(ps_t[:, bi, j * n:(j + 1) * n], hn[0:n, (c0 + j) * 128:(c0 + j + 1) * 128], idf[0:n, 0:n]) for j in range(4)]

                    def fn(e, trs=trs):
                        ins = None
                        for (o_, i_, d_) in trs:
                            ins = e.transpose(o_, i_, d_)
                        return ins
                    P.op("tensor", fn, reads=[bhn, Bidf], writes=[banks[bi]])
                    src_ = ps_t[:, bi, 0:4 * n].rearrange("p (j n) -> p j n", j=4)
                    dst_ = hts[:, c0:c0 + 4, 0:n]
                    if (c0 // 4) % 2 == 0:
                        X(P, "vector", "tensor_copy", (dst_, src_), reads=[banks[bi]], pwrites=[bhts])
                    else:
                        X(P, "scalar", "copy", (dst_, src_), reads=[banks[bi]], pwrites=[bhts])
                P.dma("scalar", HT[t // 512, :, :, (t % 512):(t % 512) + n], hts[:, :, 0:n], dhts, reads=[bhts])
        P.run()
        st.close()

    def phase6():
        P = Phase(K, "p6")
        st = ExitStack()
        hTs = sb(st, "hTs", [128, 32, 1056], BF16); B_h = Buf()
        WR = Ring(st, "wr6", 2, [128, 32, 512], BF16)
        STF = Ring(st, "stf6", 4, [128, 512], F32)
        STB = Ring(st, "stb6", 4, [128, 512], BF16)
        B_hp = [Buf() for _ in range(3)]
        for (h0, hn) in cfg.get('halves', HALVES):
            tts = ttiles(h0, hn)
            for i_, (t, n) in enumerate(tts):
                P.dma("sync", hTs[:, :, t - h0:t - h0 + n], HT[t // 512, :, :, 0:n], K.new_dsem(),
                      writes=[B_hp[i_]])
            for ft in cfg.get('p6ft', range(32)):
                wt, bw, dw = WR.nxt()
                P.dma("gpsimd", wt[:], wview(wu, ft * 512, 512), dw, writes=[bw])
                for i_, (t, n) in enumerate(tts):
                    for j in range(4):
                        ch = ft * 4 + j
                        bi = nbank(P)
                        MM(P, bi, ps_t[:, bi, 0:n], [(wt[:, k, j * 128:(j + 1) * 128], hTs[:, k, t - h0:t - h0 + n]) for k in range(32)],
                           reads=[bw, B_hp[i_]])
                        f, bf, _ = STF.nxt()
                        X(P, "scalar", "activation", (f[:, 0:n], ps_t[:, bi, 0:n], AF.Relu), reads=[banks[bi]], writes=[bf])
                        s_, bs, ds = STB.nxt()
                        X(P, "vector", "tensor_tensor", (s_[:, 0:n], f[:, 0:n], f[:, 0:n], ALU.mult), reads=[bf], writes=[bs])
                        P.dma("sync", UT[t // 512, :, ch, 0:n], s_[:, 0:n], ds, reads=[bs])
        P.run()
        st.close()

    def phase7():
        P = Phase(K, "p7")
        st = ExitStack()
        acc = sb(st, "acc", [128, 17, 512], F32)
        Bacc = [Buf() for _ in range(17)]
        WR = Ring(st, "wr7", 2, [128, 32, 512], BF16)
        UTR = Ring(st, "utr", 3, [128, 32, 512], BF16)
        HHR = Ring(st, "hhr", 2, [128, 512], F32)
        OUT = Ring(st, "out7", 3, [128, 512], F32)
        alltiles = ttiles(0, NTOK)
        for cb in cfg.get('p7cb', range(8)):
            for kg in range(4):
                wt, bw, dw = WR.nxt()
                P.dma("gpsimd", wt[:], wview(wd, cb * 512, 512, kg * 32, 32), dw, writes=[bw])
                for (t, n) in alltiles:
                    ut, but, dut = UTR.nxt()
                    P.dma("sync", ut[:, :, 0:n], UT[t // 512, :, kg * 32:(kg + 1) * 32, 0:n], dut, writes=[but])
                    for tb in range(t, t + n, 128):
                        nb_ = min(128, t + n - tb)
                        bx = tb // 128
                        bi = nbank(P)
                        MM(P, bi, ps_t[0:nb_, bi, 0:512], [(ut[:, k, tb - t:tb - t + nb_], wt[:, k, :]) for k in range(32)], reads=[bw, but])
                        if kg == 0:
                            X(P, "scalar", "copy", (acc[0:nb_, bx, :], ps_t[0:nb_, bi, 0:512]), reads=[banks[bi]], writes=[Bacc[bx]])
                        elif kg < 3:
                            X(P, "vector", "tensor_tensor", (acc[0:nb_, bx, :], acc[0:nb_, bx, :], ps_t[0:nb_, bi, 0:512], ALU.add),
                              reads=[banks[bi], Bacc[bx]], writes=[Bacc[bx]])
                        else:
                            hh, bhh, dhh = HHR.nxt()
                            P.dma("sync", hh[0:nb_, :], HH[tb:tb + nb_, cb * 512:(cb + 1) * 512], dhh, writes=[bhh])
                            o_, bo_, do_ = OUT.nxt()
                            X(P, "vector", "tensor_tensor", (o_[0:nb_, :], acc[0:nb_, bx, :], ps_t[0:nb_, bi, 0:512], ALU.add),
                              reads=[banks[bi], Bacc[bx]], writes=[bo_])
                            X(P, "vector", "scalar_tensor_tensor", (o_[0:nb_, :], hh[0:nb_, :], ALPHA, o_[0:nb_, :], ALU.mult, ALU.add),
                              reads=[bhh, bo_], writes=[bo_])
                            P.dma("scalar", FPRE[tb:tb + nb_, cb * 512:(cb + 1) * 512], o_[0:nb_, :], do_, reads=[bo_])
        P.run()
        st.close()

    if cfg.get('p1', True):
        phase1()
    if cfg.get('p2', True):
        phase2()
    if cfg.get('p3', True):
        phase3()
    if cfg.get('p4', True):
        phase4()
    if cfg.get('p5', True):
        ln_phase("p5", HPRE, 0, HH, True)
    if cfg.get('p6', True):
        phase6()
    if cfg.get('p7', True):
        phase7()
    if cfg.get('p8', True):
        ln_phase("p8", FPRE, 2, y, False)
    return nc, stack


def _host_prep(inp):
    w_in = inp["w_in"][0]
    o_mq, o_mk, o_mv, o_mo, o_mi, o_mf = 0, 1024, 2048, 4096, 6144, 6152
    o_aq, o_ak, o_av, o_gp = 6160, 8208, 8464, 8720
    cols = []
    cols += list(range(o_mq, o_mq + 1024))
    cols += list(range(o_mk, o_mk + 1024))
    cols += list(range(o_aq, o_aq + 2048))
    for kh in range(4):
        base = o_ak + kh * 64
        cols += list(range(base, base + 64)) * 2
    cols += list(range(o_gp, o_gp + 8192))
    cols = np.asarray(cols)
    assert cols.size == 12800
    w1f = np.ascontiguousarray(w_in[:, cols])
    wg = np.ascontiguousarray(w_in[:, o_mi:o_mi + 16])
    tcols = np.concatenate([np.arange(o_mv, o_mv + 2048), np.arange(o_mo, o_mo + 2048), np.arange(o_av, o_av + 256)])
    w1t = np.ascontiguousarray(w_in[:, tcols])
    half = 32
    inv = (np.float32(10000.0) ** (-(np.arange(half, dtype=np.float32) / np.float32(half)))).astype(np.float32)
    pos = np.concatenate([np.arange(S), 1024 + np.arange(TS)]).astype(np.float32)
    ang = (pos[None, :] * inv[:, None]).astype(np.float32)
    cos = np.cos(ang.astype(np.float64)).astype(np.float32)
    sin = np.sin(ang.astype(np.float64)).astype(np.float32)
    cos64 = np.concatenate([cos, cos], 0)
    sin64 = np.concatenate([-sin, sin], 0)
    cosT = np.ascontiguousarray(np.concatenate([cos64, cos64], 0))
    sinT = np.ascontiguousarray(np.concatenate([sin64, sin64], 0))
    ident = np.eye(128, dtype=np.float32)
    swap_idx = np.arange(128).reshape(2, 2, 32)[:, ::-1, :].reshape(128)
    permb = np.ascontiguousarray(ident[swap_idx].astype(ml_dtypes.bfloat16))
    maskT = np.triu(np.ones((128, 128), np.float32))
    shared = dict(
        w1f=w1f, wg=wg, w1t=w1t,
        bigf=np.ascontiguousarray(np.stack([inp["b_igate"][0], inp["b_fgate"][0]], 1)),
        wmn=inp["w_mnorm"].reshape(1, 2048),
        wa=inp["w_branch_a"][0], wb=inp["w_branch_b"][0], wo=inp["w_out"][0], wu=inp["w_up"][0], wd=inp["w_down"][0],
        lnp=np.ascontiguousarray(np.stack([inp["ln1_g"][0], inp["ln1_b"][0], inp["ln2_g"][0], inp["ln2_b"][0]], 0)),
        sinkr=np.ascontiguousarray(inp["attn_sink"].reshape(4, 4, 2).transpose(0, 2, 1).reshape(1, 32)),
        identf=ident, identb=ident.astype(ml_dtypes.bfloat16), permb=permb, maskT=maskT, cosT=cosT, sinT=sinT,
    )
    maps = []
    for b in range(8):
        m = dict(shared)
        m["x"] = np.ascontiguousarray(np.concatenate([inp["x_prompt"][b], inp["x_sample"][b]], 0))
        m["ck"] = np.ascontiguousarray(inp["cache_swa_k"][0, b].reshape(128, 256))
        m["cv"] = np.ascontiguousarray(inp["cache_swa_v"][0, b].reshape(128, 256))
        m["stC"] = np.ascontiguousarray(inp["state_mlstm_C"][0, b])
        m["stn"] = np.ascontiguousarray(inp["state_mlstm_n"][0, b])
        m["stm"] = np.ascontiguousarray(inp["state_mlstm_m"][0, b].reshape(8, 1))
        maps.append(m)
    return maps


def kernel(**inputs):
    inp = {k: np.asarray(v) for k, v in inputs.items() if not k.startswith('_')}
    maps = _host_prep(inp)
    dbg = bool(os.environ.get("MK_DEBUG"))
    nc, stack = build(debug=dbg, cfg=inputs.get("_cfg") if dbg else None)
    if dbg:
        res = run_bass_kernel_spmd(nc, maps[:1], core_ids=[0])
    else:
        res = run_bass_kernel_spmd(nc, maps, core_ids=list(range(8)))
    stack.close()
    R = res.results
    if os.environ.get("MK_DEBUG"):
        return R
    g = lambda k: np.stack([np.asarray(r[k]) for r in R], 0)
    yy = g("y")
    outs = (
        yy[:, :S, :], yy[:, S:, :],
        g("pk").reshape(1, 8, 128, 4, 64), g("pv").reshape(1, 8, 128, 4, 64),
        g("pC").reshape(1, 8, 8, 256, 128), g("pn").reshape(1, 8, 8, 128), g("pm").reshape(1, 8, 8),
        g("sk").reshape(1, 8, 32, 4, 64), g("sv").reshape(1, 8, 32, 4, 64),
        g("sC").reshape(1, 8, 8, 256, 128), g("sn").reshape(1, 8, 8, 128), g("sm").reshape(1, 8, 8),
    )
    return tuple(np.ascontiguousarray(o.astype(np.float32)) for o in outs)
```
